# Optimizing a Trainium2 kernel written in Bass

```python
import math
import jax, jax.numpy as jnp
from jax import lax
import numpy as np

D_MODEL = 1024
BATCH = 16
SEQ = 4096
DEPTH = 1
DEC_BATCH = 4
DEC_SEQ = 8192
PAST_LEN = 128

GRID_W = 64
N_Q_HEADS = 8
N_KV_HEADS = 2
HEAD_DIM = 64
Q_REP = N_Q_HEADS // N_KV_HEADS
Q_BLOCK = 128
ROPE_THETA = 10000.0
SSM_HEADS = 8
SSM_HEAD_DIM = 64
SSM_INNER = SSM_HEADS * SSM_HEAD_DIM
SSM_GROUPS = 2
HEADS_PER_GROUP = SSM_HEADS // SSM_GROUPS
D_STATE = 128
D_CONV = 5
CHUNK = 128
ATTN_DIM = N_Q_HEADS * HEAD_DIM
KV_DIM = N_KV_HEADS * HEAD_DIM
BC_DIM = SSM_GROUPS * D_STATE
CONV_DIM = SSM_INNER + 2 * BC_DIM
MIX_DIM = ATTN_DIM + SSM_INNER
IN_DIM = ATTN_DIM + 2 * KV_DIM + SSM_INNER + CONV_DIM + 2 * SSM_HEADS
D_FF = ((8 * D_MODEL + 3 * 256 - 1) // (3 * 256)) * 256
N_MOD = 6
EPS = 1e-6

kernel_name = "hymba_attn_ssd_adaln_encoder"


def rms_norm(x, g):
    xf = x.astype(jnp.float32)
    y = xf * lax.rsqrt(jnp.mean(xf * xf, axis=-1, keepdims=True) + EPS)
    return y.astype(x.dtype) * g


def _rope_1d(xa, pos):
    half = xa.shape[-1] // 2
    freqs = ROPE_THETA ** (-jnp.arange(half, dtype=jnp.float32) / half)
    ang = pos.astype(jnp.float32)[:, None] * freqs[None, :]
    cos = jnp.cos(ang)[None, :, None, :].astype(xa.dtype)
    sin = jnp.sin(ang)[None, :, None, :].astype(xa.dtype)
    x1, x2 = xa[..., :half], xa[..., half:]
    return jnp.concatenate([x1 * cos - x2 * sin, x1 * sin + x2 * cos], axis=-1)


def axial_rope(x, rows, cols):
    ra = HEAD_DIM // 2
    return jnp.concatenate([_rope_1d(x[..., :ra], rows), _rope_1d(x[..., ra:], cols)], axis=-1)


def block_attention(q, k, v):
    b, s = q.shape[0], q.shape[1]
    nb = s // Q_BLOCK
    scale = 1.0 / math.sqrt(HEAD_DIM)
    qb = q.reshape(b, nb, Q_BLOCK, N_KV_HEADS, Q_REP, HEAD_DIM).transpose(1, 0, 2, 3, 4, 5)

    def one_block(qblk):
        scores = jnp.einsum("bqgrd,bkgd->bgrqk", qblk, k, preferred_element_type=jnp.float32) * scale
        p = jax.nn.softmax(scores, axis=-1).astype(v.dtype)
        return jnp.einsum("bgrqk,bkgd->bqgrd", p, v)

    out = lax.map(one_block, qb)
    return out.transpose(1, 0, 2, 3, 4, 5).reshape(b, s, ATTN_DIM)


def segsum(a):
    cs = jnp.cumsum(a, axis=-1)
    diff = cs[..., :, None] - cs[..., None, :]
    t = a.shape[-1]
    mask = jnp.tril(jnp.ones((t, t), dtype=bool))
    return jnp.where(mask, diff, -jnp.inf)


def ssd_scan(x, dt, a, bm, cm):
    b, s = x.shape[0], x.shape[1]
    nc = s // CHUNK
    dtype = x.dtype
    xdt = (x.astype(jnp.float32) * dt[..., None]).astype(dtype)
    xdt = xdt.reshape(b, nc, CHUNK, SSM_HEADS, SSM_HEAD_DIM)
    bc = bm.reshape(b, nc, CHUNK, SSM_HEADS, D_STATE)
    cc = cm.reshape(b, nc, CHUNK, SSM_HEADS, D_STATE)
    a_dt = (dt * a.astype(jnp.float32)).reshape(b, nc, CHUNK, SSM_HEADS).transpose(0, 3, 1, 2)
    a_cs = jnp.cumsum(a_dt, axis=-1)
    decay_in = jnp.exp(segsum(a_dt)).astype(dtype)
    cb = jnp.einsum("bclhn,bcshn->bhcls", cc, bc)
    y_diag = jnp.einsum("bhcls,bcshp->bclhp", cb * decay_in, xdt)
    decay_states = jnp.exp(a_cs[..., -1:] - a_cs).astype(dtype)
    states = jnp.einsum("bclhn,bhcl,bclhp->bchpn", bc, decay_states, xdt)
    states = jnp.concatenate([jnp.zeros_like(states[:, :1]), states], axis=1)
    chunk_a = jnp.pad(a_cs[..., -1], ((0, 0), (0, 0), (1, 0)))
    decay_chunk = jnp.exp(segsum(chunk_a)).astype(dtype)
    states = jnp.einsum("bhzc,bchpn->bzhpn", decay_chunk, states)[:, :-1]
    decay_out = jnp.exp(a_cs).astype(dtype)
    y_off = jnp.einsum("bclhn,bchpn,bhcl->bclhp", cc, states, decay_out)
    return (y_diag + y_off).reshape(b, s, SSM_HEADS, SSM_HEAD_DIM)


def encoder_layer(x, c, w_mod, b_mod, norm_mix_g, w_in, q_norm_g, k_norm_g, conv_w, conv_b,
                  dt_bias_fwd, a_log_fwd, dt_bias_bwd, a_log_bwd, d_skip, ssd_norm_g, w_out,
                  norm_ffn_g, w_gate_up, w_down):
    b, s, _ = x.shape
    n_rows = s // GRID_W
    rows = jnp.repeat(jnp.arange(n_rows), GRID_W)
    cols = jnp.tile(jnp.arange(GRID_W), n_rows)

    mod = jax.nn.silu(c) @ w_mod + b_mod
    shift1, scale1, gate1, shift2, scale2, gate2 = [m[:, None, :] for m in jnp.split(mod, N_MOD, axis=-1)]

    h = rms_norm(x, norm_mix_g) * (1 + scale1) + shift1
    proj = h @ w_in
    sizes = [ATTN_DIM, KV_DIM, KV_DIM, SSM_INNER, CONV_DIM, SSM_HEADS, SSM_HEADS]
    parts, off = [], 0
    for sz in sizes:
        parts.append(proj[..., off:off + sz])
        off += sz
    q, k, v, z, xbc, dt_f, dt_b = parts

    q = rms_norm(q.reshape(b, s, N_Q_HEADS, HEAD_DIM), q_norm_g)
    k = rms_norm(k.reshape(b, s, N_KV_HEADS, HEAD_DIM), k_norm_g)
    v = v.reshape(b, s, N_KV_HEADS, HEAD_DIM)
    q = axial_rope(q, rows, cols)
    k = axial_rope(k, rows, cols)
    attn_out = block_attention(q, k, v)

    pad = D_CONV // 2
    xbc = lax.conv_general_dilated(xbc, conv_w[:, None, :], window_strides=(1,), padding=[(pad, pad)],
                                   dimension_numbers=("NWC", "WIO", "NWC"), feature_group_count=CONV_DIM)
    xbc = jax.nn.silu(xbc + conv_b)
    xs = xbc[..., :SSM_INNER].reshape(b, s, SSM_HEADS, SSM_HEAD_DIM)
    bm = jnp.repeat(xbc[..., SSM_INNER:SSM_INNER + BC_DIM].reshape(b, s, SSM_GROUPS, D_STATE), HEADS_PER_GROUP, axis=2)
    cm = jnp.repeat(xbc[..., SSM_INNER + BC_DIM:].reshape(b, s, SSM_GROUPS, D_STATE), HEADS_PER_GROUP, axis=2)
    dtf = jax.nn.softplus(dt_f.astype(jnp.float32) + dt_bias_fwd.astype(jnp.float32))
    dtb = jax.nn.softplus(dt_b.astype(jnp.float32) + dt_bias_bwd.astype(jnp.float32))
    a_f = -jnp.exp(a_log_fwd.astype(jnp.float32))
    a_b = -jnp.exp(a_log_bwd.astype(jnp.float32))
    y_fwd = ssd_scan(xs, dtf, a_f, bm, cm)
    y_bwd = ssd_scan(xs[:, ::-1], dtb[:, ::-1], a_b, bm[:, ::-1], cm[:, ::-1])[:, ::-1]
    y_ssd = (y_fwd + y_bwd + d_skip[:, None] * xs).reshape(b, s, SSM_INNER)
    ssd_out = rms_norm(y_ssd * jax.nn.silu(z), ssd_norm_g)

    mix = jnp.concatenate([attn_out, ssd_out], axis=-1) @ w_out
    x = x + gate1 * mix

    h2 = rms_norm(x, norm_ffn_g) * (1 + scale2) + shift2
    gu = h2 @ w_gate_up
    ffn = (jax.nn.silu(gu[..., :D_FF]) * gu[..., D_FF:]) @ w_down
    return x + gate2 * ffn


def setup_inputs(seed: int = 0) -> dict:
    key = jax.random.key(seed)
    ks = jax.random.split(key, 24)
    f32 = jnp.float32
    D = D_MODEL

    def nrm(k, shape, scale):
        return jax.random.normal(k, shape, f32) * scale

    dt0_f = jnp.exp(jax.random.uniform(ks[10], (DEPTH, SSM_HEADS), f32, math.log(1e-3), math.log(1e-1)))
    dt0_b = jnp.exp(jax.random.uniform(ks[11], (DEPTH, SSM_HEADS), f32, math.log(1e-3), math.log(1e-1)))
    return {
        "x_prompt": nrm(ks[0], (BATCH, SEQ, D), 1.0),
        "x_sample": nrm(ks[1], (DEC_BATCH, DEC_SEQ, D), 1.0),
        "c_prompt": nrm(ks[2], (BATCH, D), 1.0),
        "c_sample": nrm(ks[3], (DEC_BATCH, D), 1.0),
        "w_mod": nrm(ks[4], (DEPTH, D, N_MOD * D), 0.5 * D ** -0.5),
        "b_mod": nrm(ks[5], (DEPTH, N_MOD * D), 0.02),
        "norm_mix_g": 1.0 + nrm(ks[6], (DEPTH, D), 0.02),
        "w_in": nrm(ks[7], (DEPTH, D, IN_DIM), D ** -0.5),
        "q_norm_g": 1.0 + nrm(ks[8], (DEPTH, HEAD_DIM), 0.02),
        "k_norm_g": 1.0 + nrm(ks[9], (DEPTH, HEAD_DIM), 0.02),
        "conv_w": nrm(ks[12], (DEPTH, D_CONV, CONV_DIM), D_CONV ** -0.5),
        "conv_b": nrm(ks[13], (DEPTH, CONV_DIM), 0.02),
        "dt_bias_fwd": dt0_f + jnp.log(-jnp.expm1(-dt0_f)),
        "a_log_fwd": jnp.log(jax.random.uniform(ks[14], (DEPTH, SSM_HEADS), f32, 1.0, 16.0)),
        "dt_bias_bwd": dt0_b + jnp.log(-jnp.expm1(-dt0_b)),
        "a_log_bwd": jnp.log(jax.random.uniform(ks[15], (DEPTH, SSM_HEADS), f32, 1.0, 16.0)),
        "d_skip": 1.0 + nrm(ks[16], (DEPTH, SSM_HEADS), 0.1),
        "ssd_norm_g": 1.0 + nrm(ks[17], (DEPTH, SSM_INNER), 0.02),
        "w_out": nrm(ks[18], (DEPTH, MIX_DIM, D), MIX_DIM ** -0.5),
        "norm_ffn_g": 1.0 + nrm(ks[19], (DEPTH, D), 0.02),
        "w_gate_up": nrm(ks[20], (DEPTH, D, 2 * D_FF), D ** -0.5),
        "w_down": nrm(ks[21], (DEPTH, D_FF, D), D_FF ** -0.5),
    }


def reference(x_prompt, x_sample, c_prompt, c_sample, w_mod, b_mod, norm_mix_g, w_in, q_norm_g,
              k_norm_g, conv_w, conv_b, dt_bias_fwd, a_log_fwd, dt_bias_bwd, a_log_bwd, d_skip,
              ssd_norm_g, w_out, norm_ffn_g, w_gate_up, w_down):
    def run(x, c):
        for l in range(DEPTH):
            x = encoder_layer(x, c, w_mod[l], b_mod[l], norm_mix_g[l], w_in[l], q_norm_g[l], k_norm_g[l],
                              conv_w[l], conv_b[l], dt_bias_fwd[l], a_log_fwd[l], dt_bias_bwd[l],
                              a_log_bwd[l], d_skip[l], ssd_norm_g[l], w_out[l], norm_ffn_g[l],
                              w_gate_up[l], w_down[l])
        return x

    y_prompt = run(x_prompt, c_prompt)
    y_sample = run(x_sample, c_sample)
    return (y_prompt, y_sample)
```

```python
import contextlib
import numpy as np
import concourse.bass as bass
import concourse.mybir as mybir
from concourse.bass_utils import run_bass_kernel_spmd

F32 = mybir.dt.float32
BF16 = mybir.dt.bfloat16
AF = mybir.ActivationFunctionType
ALU = mybir.AluOpType

D = 1024
NH = 8
EPS = 1e-6
NDS = 16
NHW = 12
import itertools
_uid = itertools.count()


class Buf:
    __slots__ = ("w", "r", "acc")

    def __init__(self, acc=False):
        self.w = {}
        self.r = {}
        self.acc = acc


class Prog:
    def __init__(self, nc, es):
        self.nc = nc
        self.eng = {"pe": nc.tensor, "act": nc.scalar, "dve": nc.vector, "pool": nc.gpsimd, "sp": nc.sync}
        self.semobj = {}
        for k in ["pe", "act", "dve", "pool"]:
            self.semobj[k] = es.enter_context(nc.semaphore("sem_" + k))
        for i in range(NDS):
            self.semobj[("d", i)] = es.enter_context(nc.semaphore("dsem%d" % i))
        self.cnt = {k: 0 for k in ["pe", "act", "dve", "pool"]}
        self.seen = {k: {} for k in self.eng}
        self.dval = [0] * NDS
        self.dnext = 0
        self.dnext_sw = 0
        self.nwait = 0
        self.nins = 0
        self.log = []

    def _wait(self, ek, sk, v):
        if self.seen[ek].get(sk, 0) >= v:
            return
        if sk == ek and ek == "pe":
            return
        self.eng[ek].wait_ge(self.semobj[sk], v)
        self.log.append((ek, 'WAIT', sk, v))
        self.seen[ek][sk] = v
        self.nwait += 1

    def _deps(self, ek, reads, writes):
        for b in reads:
            for sk, v in b.w.items():
                self._wait(ek, sk, v)
        for b in writes:
            if not b.acc:
                for sk, v in b.w.items():
                    self._wait(ek, sk, v)
            for sk, v in b.r.items():
                self._wait(ek, sk, v)

    def _mark(self, tok, reads, writes):
        sk, v = tok
        for b in reads:
            if b.r.get(sk, 0) < v:
                b.r[sk] = v
        for b in writes:
            if b.acc:
                if b.w.get(sk, 0) < v:
                    b.w[sk] = v
            else:
                b.w = {sk: v}
                b.r = {}

    def op(self, ek, fn, reads=(), writes=(), inc=True):
        self._deps(ek, reads, writes)
        ins = fn(self.eng[ek])
        self.nins += 1
        tok = (ek, self.cnt[ek] + 1)
        self.log.append((ek, 'OP', self.cnt[ek] + 1, inc))
        if inc:
            ins.then_inc(self.semobj[ek], 1)
            self.cnt[ek] += 1
        self._mark(tok, reads, writes)

    def dma(self, qk, out, in_, reads=(), writes=(), **kw):
        self._deps(qk, reads, writes)
        if qk == "pool":
            i = NHW + self.dnext_sw
            self.dnext_sw = (self.dnext_sw + 1) % (NDS - NHW)
        else:
            i = self.dnext
            self.dnext = (i + 1) % NHW
        if self.dval[i] > 0:
            self._wait(qk, ("d", i), self.dval[i])
        self.dval[i] += 16
        self.eng[qk].dma_start(out=out, in_=in_, **kw).then_inc(self.semobj[("d", i)], 16)
        self.nins += 1
        self.log.append((qk, 'DMA', i, self.dval[i]))
        self._mark((("d", i), self.dval[i]), reads, writes)

    def barrier(self):
        self.log.append(('*', 'BARRIER', 0, 0))
        for ek in ["pe", "act", "dve", "pool", "sp"]:
            for i in range(NDS):
                if self.dval[i] > 0:
                    self._wait(ek, ("d", i), self.dval[i])
            for k in ["pe", "act", "dve", "pool"]:
                if k != ek and self.cnt[k] > 0:
                    self._wait(ek, k, self.cnt[k])

    def finish(self, ek="sp"):
        for i in range(NDS):
            if self.dval[i] > 0:
                self._wait(ek, ("d", i), self.dval[i])
        for k in ["pe", "act", "dve", "pool"]:
            if self.cnt[k] > 0:
                self._wait(ek, k, self.cnt[k])

    def mm(self, out, lhsT, rhs, start, stop, R, W, inc=True):
        self.op("pe", lambda e: e.matmul(out, lhsT=lhsT, rhs=rhs, start=start, stop=stop), R, W, inc)

    def tr(self, out, in_, ident, R, W, inc=True):
        self.op("pe", lambda e: e.transpose(out, in_, ident), R, W, inc)

    def act(self, out, in_, func, R, W, scale=None, bias=None, accum=None):
        kw = {}
        if scale is not None:
            kw["scale"] = scale
        if bias is not None:
            kw["bias"] = bias
        if accum is not None:
            kw["accum_out"] = accum
        self.op("act", lambda e: e.activation(out=out, in_=in_, func=func, **kw), R, W)

    def tt(self, ek, out, in0, in1, op, R, W):
        self.op(ek, lambda e: e.tensor_tensor(out=out, in0=in0, in1=in1, op=op), R, W)

    def ts(self, ek, out, in0, s1, s2, op0, op1, R, W):
        if op1 is None:
            self.op(ek, lambda e: e.tensor_scalar(out=out, in0=in0, scalar1=s1, scalar2=None, op0=op0), R, W)
        else:
            self.op(ek, lambda e: e.tensor_scalar(out=out, in0=in0, scalar1=s1, scalar2=s2, op0=op0, op1=op1), R, W)

    def stt(self, out, in0, scalar, in1, op0, op1, R, W):
        self.op("dve", lambda e: e.scalar_tensor_tensor(out=out, in0=in0, scalar=scalar, in1=in1, op0=op0, op1=op1), R, W)

    def copy(self, ek, out, in_, R, W):
        if ek == "act":
            self.op(ek, lambda e: e.copy(out=out, in_=in_), R, W)
        else:
            self.op(ek, lambda e: e.tensor_copy(out=out, in_=in_), R, W)

    def memset(self, ek, ap, val, W):
        self.op(ek, lambda e: e.memset(ap, val), (), W)


class TB:
    def __init__(self, t, acc=False):
        self.t = t
        self.b = Buf(acc)

    def __getitem__(self, idx):
        return self.t[idx]


COL_QG, COL_KG, COL_GMIX, COL_GFFN, COL_CB, COL_CW, NCOLP = 0, 1, 2, 10, 18, 26, 66
ROW_DTB, ROW_ALOG, ROW_DSKIP, ROW_GSSD, NROWP = 0, 16, 32, 544, 1056
C_ID, C_BONES, C_RPERM, C_TRIUP, C_TRIDN, C_ONES, C_SEL, C_SELB, NCONST = 0, 1, 2, 3, 4, 5, 6, 7, 8
NFM = 13
TM0 = NFM * 128
NIN = 2320
DFF = 2816


def build(SP, SO, SC, NP=2, debug=False, phases="ABCD"):
    nc = bass.Bass("TRN2", target_bir_lowering=False)
    segs = [(SP, 0)] * NP + [(SO, SC)]
    NSEG = len(segs)
    offs = [sum(s[0] for s in segs[:i]) for i in range(NSEG)]
    TOT = sum(s[0] for s in segs)
    SKMAX = max(s[0] + s[1] for s in segs)

    def din(name, shape, dt=F32):
        return nc.dram_tensor(name, list(shape), dt, kind="ExternalInput").ap()

    def dscr(name, shape, dt):
        kind = "ExternalOutput" if debug else "Internal"
        return TB(nc.dram_tensor(name, list(shape), dt, kind=kind).ap(), acc=True)

    x_own = din("x_own", [TOT, D])
    x_ctx = din("x_ctx", [max(SC, 128), D])
    cT_d = din("cT", [128, 8, NSEG])
    w_mod = din("w_mod", [D, 6 * D])
    b_mod3 = din("b_mod3", [NSEG, 6 * D])
    w_in = din("w_in", [D, NIN])
    w_out = din("w_out", [D, D])
    w_gu = din("w_gu", [D, 2 * DFF])
    w_dn = din("w_dn", [DFF, D])
    pcol_d = din("pcol", [128, NCOLP])
    prow_d = din("prow", [128, NROWP])
    const_d = din("consts", [128, NCONST * 128])
    sel3_d = din("sel3", [NSEG, NSEG * 128])
    rope_p = din("rope_p", [2, 64, SP])
    rope_s = din("rope_s", [2, 64, SO + SC])
    y_out = TB(nc.dram_tensor("y_own", [TOT, D], F32, kind="ExternalOutput").ap(), acc=True)

    w_in_b = dscr("w_in_b", [D, NIN], BF16)
    w_out_b = dscr("w_out_b", [D, D], BF16)
    w_gu_b = dscr("w_gu_b", [D, 2 * DFF], BF16)
    w_dn_b = dscr("w_dn_b", [DFF, D], BF16)
    mod_d = dscr("mod_d", [NSEG, 6 * D], F32)
    S = []
    for si, (so, sc) in enumerate(segs):
        st = so + sc
        S.append(dict(
            QT=dscr("QT%d" % si, [512, so], BF16),
            KT=dscr("KT%d" % si, [128, st], BF16),
            VA=dscr("VA%d" % si, [st, 258], BF16),
            XBC=dscr("XBC%d" % si, [1024, st + 4], BF16),
            SZ=dscr("SZ%d" % si, [so, 512], BF16),
            DT=dscr("DT%d" % si, [st, 16], F32),
            MIXT=dscr("MIXT%d" % si, [1024, so], BF16),
            BCT=dscr("BCT%d" % si, [512, st], BF16),
            XTOK=dscr("XTOK%d" % si, [st, 768], BF16),
            YDN=dscr("YDN%d" % si, [so, 512], F32),
        ))

    top = contextlib.ExitStack()
    with top:
        p = Prog(nc, top)

        def sb(es, name, shape, dt):
            return TB(es.enter_context(nc.sbuf_tensor("s%d_%s" % (next(_uid), name), list(shape), dt)))

        def ps(es, name, shape, dt):
            return TB(es.enter_context(nc.psum_tensor("p%d_%s" % (next(_uid), name), list(shape), dt)))

        cst = sb(top, "cst", [128, NCONST * 128], F32)
        cstb = sb(top, "cstb", [128, NCONST * 128], BF16)
        pcol = sb(top, "pcol", [128, NCOLP], F32)
        prow = sb(top, "prow", [128, NROWP], F32)
        epsc = sb(top, "epsc", [128, 2], F32)
        p.dma("sp", cst[:], const_d, (), [cst.b])
        p.dma("sp", pcol[:], pcol_d, (), [pcol.b])
        p.dma("sp", prow[:], prow_d, (), [prow.b])
        p.copy("dve", cstb[:], cst[:], [cst.b], [cstb.b])
        p.memset("dve", epsc[:, 0:1], EPS, [epsc.b])
        p.memset("dve", epsc[:, 1:2], 1.0, [epsc.b])
        aneg = sb(top, "aneg", [128, 16], F32)
        p.act(aneg[:], prow[:, ROW_ALOG:ROW_ALOG + 16], AF.Exp, [prow.b], [aneg.b])
        p.ts("dve", aneg[:], aneg[:], -1.0, None, ALU.mult, None, [aneg.b], [aneg.b])

        def cblk(i, bf=False):
            return (cstb if bf else cst)[:, i * 128:(i + 1) * 128]

        for (src, dst, rows) in [(w_in, w_in_b, D), (w_out, w_out_b, D), (w_gu, w_gu_b, D), (w_dn, w_dn_b, DFF)]:
            for r0 in range(0, rows, 128):
                p.dma("pool", dst[r0:r0 + 128, :], src[r0:r0 + 128, :], (), [dst.b])

        with contextlib.ExitStack() as es:
            cT = sb(es, "cTs", [128, 8, NSEG], F32)
            scT = sb(es, "scT", [128, 8, NSEG], F32)
            bm = sb(es, "bm", [NSEG, 6 * D], F32)
            wm = [sb(es, "wm%d" % i, [128, 8, 512], F32) for i in range(2)]
            mst = [sb(es, "mst%d" % i, [NSEG, 512], F32) for i in range(2)]
            mps = [ps(es, "mps%d" % i, [NSEG, 512], F32) for i in range(2)]
            p.dma("sp", cT[:], cT_d, (), [cT.b])
            p.dma("sp", bm[:], b_mod3, (), [bm.b])
            p.act(scT[:], cT[:], AF.Silu, [cT.b], [scT.b])
            wmv = w_mod.rearrange("(kc p) n -> p kc n", p=128)
            for nt in range(12):
                w = wm[nt % 2]
                p.dma("sp", w[:], wmv[:, :, nt * 512:(nt + 1) * 512], (), [w.b])
                m = mps[nt % 2]
                for kc in range(8):
                    p.mm(m[:], scT[:, kc, :], w[:, kc, :], kc == 0, kc == 7, [scT.b, w.b], [m.b], inc=(kc == 7))
                s_ = mst[nt % 2]
                p.tt("dve", s_[:], m[:], bm[:, nt * 512:(nt + 1) * 512], ALU.add, [m.b, bm.b], [s_.b])
                p.dma("sp", mod_d[:, nt * 512:(nt + 1) * 512], s_[:], [s_.b], [mod_d.b])
            p.barrier()

        modc = [sb(top, "modc%d" % si, [128, 4, 8], F32) for si in range(NSEG)]
        with contextlib.ExitStack() as es:
            tmpc = sb(es, "tmpc", [128, 4, 8], F32)
            for si in range(NSEG):
                mc = modc[si]
                for qi, q in enumerate([1, 0, 4, 3]):
                    src = mod_d[si, q * D:(q + 1) * D].rearrange("(kc p) -> p kc", p=128)
                    p.dma("sp", tmpc[:, qi, :], src, [mod_d.b], [tmpc.b], allow_slow_non_contiguous=True)
                for (qi, gcol, oi) in [(0, COL_GMIX, 0), (2, COL_GFFN, 2)]:
                    p.ts("dve", mc[:, oi, :], tmpc[:, qi, :], 1.0, None, ALU.add, None, [tmpc.b], [mc.b])
                    p.tt("dve", mc[:, oi, :], mc[:, oi, :], pcol[:, gcol:gcol + 8], ALU.mult, [mc.b, pcol.b], [mc.b])
                p.copy("dve", mc[:, 1, :], tmpc[:, 1, :], [tmpc.b], [mc.b])
                p.copy("dve", mc[:, 3, :], tmpc[:, 3, :], [tmpc.b], [mc.b])
            p.barrier()

        for si, (so, sc) in enumerate(segs):
            st = so + sc
            sd = S[si]
            rope = rope_p if si < NP else rope_s
            p.barrier()
            if "A" in phases:
                phase_A(nc, p, sb, ps, si, so, sc, offs[si], x_own, x_ctx, w_in_b, sd, rope, modc[si],
                        pcol, prow, cst, cstb, epsc)
            p.barrier()
            if "B" in phases:
                phase_B(nc, p, sb, ps, si, so, sc, sd, cst)
            p.barrier()
            if "C" not in phases and "D" in phases:
                with contextlib.ExitStack() as es:
                    zt = sb(es, "zt", [128, 512], BF16)
                    p.memset("dve", zt[:], 0.0, [zt.b])
                    for r0 in range(512, 1024, 128):
                        for c0 in range(0, so, 512):
                            p.dma("sp", sd["MIXT"][r0:r0 + 128, c0:c0 + 512], zt[:], [zt.b], [sd["MIXT"].b])
                p.barrier()
            if "C" in phases:
                phase_C(nc, p, sb, ps, si, so, sc, sd, pcol, prow, cst, cstb, epsc, aneg)
            p.barrier()
            if "D" in phases:
                phase_D(nc, p, sb, ps, si, so, offs[si], x_own, y_out, sd, mod_d, modc[si], w_out_b, w_gu_b,
                        w_dn_b, cstb, epsc)
        p.finish("sp")
    build.last_log = p.log
    print("instructions", p.nins, "waits", p.nwait, "counts", p.cnt)
    return nc


def phase_A(nc, p, sb, ps, si, so, sc, off, x_own, x_ctx, w_in_b, sd, rope, mc, pcol, prow, cst, cstb, epsc):
    st = so + sc
    with contextlib.ExitStack() as es:
        win = sb(es, "win", [128, 8, NIN], BF16)
        wv = w_in_b.t.rearrange("(kc p) n -> p kc n", p=128)
        for kc in range(8):
            p.dma("sp", win[:, kc, :], wv[:, kc, :], [w_in_b.b], [win.b])
        xs = [sb(es, "xs%d" % i, [128, D], F32) for i in range(3)]
        junk = sb(es, "junk", [128, D], BF16)
        ssq = sb(es, "ssq", [128, 8], F32)
        xn = [sb(es, "xn%d" % i, [128, 4, D], BF16) for i in range(2)]
        hT = [sb(es, "hT%d" % i, [128, 8, 512], BF16) for i in range(2)]
        sq = [sb(es, "sq%d" % i, [128, 512], F32) for i in range(2)]
        qg = [sb(es, "qg%d" % i, [128, 512], BF16) for i in range(2)]
        lnv = [sb(es, "lnv%d" % i, [128, 512], F32) for i in range(2)]
        t1 = [sb(es, "t1%d" % i, [128, 512], F32) for i in range(2)]
        t2 = [sb(es, "t2%d" % i, [128, 512], F32) for i in range(2)]
        qo = [sb(es, "qo%d" % i, [128, 512], BF16) for i in range(3)]
        ctab = [sb(es, "ctab%d" % i, [128, 2, 512], F32) for i in range(2)]
        xbs = [sb(es, "xbs%d" % i, [128, 8, 512], BF16) for i in range(2)]
        szs = [sb(es, "szs%d" % i, [128, 4, 512], BF16) for i in range(2)]
        vas = [sb(es, "vas%d" % i, [128, 4, 2, 129], BF16) for i in range(2)]
        dts = [sb(es, "dts%d" % i, [128, 4, 16], F32) for i in range(2)]
        dtr = sb(es, "dtr", [128, 4, 16], F32)
        zpad = sb(es, "zpad", [128, 8, 2], BF16)
        tp = [ps(es, "tpA%d" % i, [128, 512], BF16) for i in range(2)]
        fm = [ps(es, "fmA%d" % i, [128, 512], F32) for i in range(2)]
        tmz = [ps(es, "tmz", [128, 512], F32)] * 2
        tmv = [ps(es, "tmv", [128, 256], F32)] * 2
        ssb = ps(es, "ssbA", [128, 512], F32)
        rot = ps(es, "rotA", [128, 512], F32)

        for v_ in vas:
            p.memset("pool", v_[:], 0.0, [v_.b])
            p.memset("pool", v_[:, :, :, 0:1], 1.0, [v_.b])
            p.memset("pool", v_[:, :, :, 128:129], 1.0, [v_.b])
        p.memset("pool", zpad[:], 0.0, [zpad.b])
        xbv = sd["XBC"].t.rearrange("(t p) s -> p t s", p=128)
        p.dma("sp", xbv[:, :, 0:2], zpad[:], [zpad.b], [sd["XBC"].b])
        p.dma("sp", xbv[:, :, st + 2:st + 4], zpad[:], [zpad.b], [sd["XBC"].b])

        ident = cstb[:, C_ID * 128:(C_ID + 1) * 128]
        nmt = st // 512
        qi = 0
        xi = 0
        def part1(mt):
            own = (mt * 512) < so
            t0 = mt * 512
            par = mt % 2
            nonlocal xi
            ct = ctab[par]
            for hh in range(2):
                p.dma("sp", ct[hh * 64:(hh + 1) * 64, :, :],
                      rope[:, :, t0:t0 + 512].rearrange("c d s -> d c s"), (), [ct.b])
            xnt = xn[par]
            for sub in range(4):
                xt = xs[xi % 3]
                xi += 1
                if own:
                    src = x_own[off + t0 + sub * 128: off + t0 + (sub + 1) * 128, :]
                else:
                    src = x_ctx[t0 - so + sub * 128: t0 - so + (sub + 1) * 128, :]
                p.dma("sp", xt[:], src, (), [xt.b])
                p.act(junk[:], xt[:], AF.Square, [xt.b], [junk.b, ssq.b], accum=ssq[:, sub:sub + 1])
                p.act(ssq[:, 4 + sub:5 + sub], ssq[:, sub:sub + 1], AF.Ln, [ssq.b, epsc.b], [ssq.b],
                      scale=1.0 / D, bias=epsc[:, 0:1])
                p.act(ssq[:, 4 + sub:5 + sub], ssq[:, 4 + sub:5 + sub], AF.Exp, [ssq.b], [ssq.b], scale=-0.5)
                p.act(xnt[:, sub, :], xt[:], AF.Copy, [xt.b, ssq.b], [xnt.b], scale=ssq[:, 4 + sub:5 + sub])
            h = hT[par]
            for kc in range(8):
                tpp = tp[kc % 2]
                for sub in range(4):
                    p.tr(tpp[:, sub * 128:(sub + 1) * 128], xnt[:, sub, kc * 128:(kc + 1) * 128], ident,
                         [xnt.b, cstb.b], [tpp.b], inc=(sub == 3))
                p.ts("dve", h[:, kc, :], tpp[:], mc[:, 0, kc:kc + 1], mc[:, 1, kc:kc + 1], ALU.mult, ALU.add,
                     [tpp.b, mc.b], [h.b])

        def part2(mt):
            own = (mt * 512) < so
            t0 = mt * 512
            par = mt % 2
            nonlocal qi
            ct = ctab[par]
            h = hT[par]
            xb = xbs[par]
            ftiles = list(range(NFM)) if own else [0] + list(range(5, NFM))
            pending = []

            def post_qk(f, j, t0=t0, ct=ct):
                p.mm(ssb[:], cst[:, C_BONES * 128:(C_BONES + 1) * 128], sq[j][:], True, True,
                     [cst.b, sq[j].b], [ssb.b])
                p.mm(rot[:], cstb[:, C_RPERM * 128:(C_RPERM + 1) * 128], qg[j][:], True, True,
                     [cstb.b, qg[j].b], [rot.b])
                p.act(lnv[j][:], ssb[:], AF.Ln, [ssb.b, epsc.b], [lnv[j].b], bias=epsc[:, 0:1])
                p.act(lnv[j][:], lnv[j][:], AF.Exp, [lnv[j].b], [lnv[j].b], scale=-0.5)
                p.tt("dve", t1[j][:], qg[j][:], ct[:, 0, :], ALU.mult, [qg[j].b, ct.b], [t1[j].b])
                p.tt("dve", t2[j][:], rot[:], ct[:, 1, :], ALU.mult, [rot.b, ct.b], [t2[j].b])
                p.tt("pool", t1[j][:], t1[j][:], t2[j][:], ALU.add, [t1[j].b, t2[j].b], [t1[j].b])
                o = qo[f % 3]
                p.tt("pool", o[:], t1[j][:], lnv[j][:], ALU.mult, [t1[j].b, lnv[j].b], [o.b])
                if f == 0:
                    p.dma("sp", sd["KT"][:, t0:t0 + 512], o[:], [o.b], [sd["KT"].b])
                else:
                    p.dma("sp", sd["QT"][(f - 1) * 128:f * 128, t0:t0 + 512], o[:], [o.b], [sd["QT"].b])

            for f in ftiles:
                acc = fm[qi % 2]
                qi += 1
                for kc in range(8):
                    p.mm(acc[:], win[:, kc, f * 128:(f + 1) * 128], h[:, kc, :], kc == 0, kc == 7,
                         [win.b, h.b], [acc.b], inc=(kc == 7))
                for (pf, pj) in pending:
                    post_qk(pf, pj)
                pending = []
                if f < 5:
                    j = f % 2
                    gcol = COL_KG if f == 0 else COL_QG
                    p.act(sq[j][:], acc[:], AF.Square, [acc.b], [sq[j].b])
                    p.act(qg[j][:], acc[:], AF.Copy, [acc.b, pcol.b], [qg[j].b], scale=pcol[:, gcol:gcol + 1])
                    pending.append((f, j))
                else:
                    eng = "act" if (f % 2) else "dve"
                    p.copy(eng, xb[:, f - 5, :], acc[:], [acc.b], [xb.b])
            for (pf, pj) in pending:
                post_qk(pf, pj)
            ntl = 8
            p.dma("sp", xbv[:, 0:ntl, 2 + t0:2 + t0 + 512], xb[:, 0:ntl, :], [xb.b], [sd["XBC"].b])
            sz = szs[par]
            va = vas[par]
            dtt = dts[par]
            for sub in range(4):
                if own:
                    az = tmz[sub % 2]
                    for kc in range(8):
                        p.mm(az[:], h[:, kc, sub * 128:(sub + 1) * 128], win[:, kc, TM0:TM0 + 512], kc == 0, kc == 7,
                             [win.b, h.b], [az.b], inc=(kc == 7))
                    p.act(sz[:, sub, :], az[:], AF.Silu, [az.b], [sz.b])
                av = tmv[sub % 2]
                for kc in range(8):
                    p.mm(av[:, 0:144], h[:, kc, sub * 128:(sub + 1) * 128], win[:, kc, TM0 + 512:TM0 + 656], kc == 0,
                         kc == 7, [win.b, h.b], [av.b], inc=(kc == 7))
                p.copy("dve", va[:, sub, :, 64:128], av[:, 0:128].rearrange("p (g d) -> p g d", g=2), [av.b], [va.b])
                p.tt("dve", dtr[:, sub, :], av[:, 128:144], prow[:, ROW_DTB:ROW_DTB + 16], ALU.add, [av.b, prow.b],
                     [dtr.b])
            p.act(dtr[:], dtr[:], AF.Exp, [dtr.b], [dtr.b])
            p.act(dtt[:], dtr[:], AF.Ln, [dtr.b, epsc.b], [dtt.b], bias=epsc[:, 1:2])
            if own:
                p.dma("sp", sd["SZ"][t0:t0 + 512, :].rearrange("(s p) c -> p s c", p=128), sz[:], [sz.b], [sd["SZ"].b])
            p.dma("sp", sd["VA"][t0:t0 + 512, :].rearrange("(s p) c -> p s c", p=128),
                  va[:].rearrange("p s g c -> p s (g c)"), [va.b], [sd["VA"].b])
            p.dma("sp", sd["DT"][t0:t0 + 512, :].rearrange("(s p) c -> p s c", p=128), dtt[:], [dtt.b], [sd["DT"].b])

        part1(0)
        for mt in range(nmt):
            if mt + 1 < nmt:
                part1(mt + 1)
            part2(mt)


def phase_B(nc, p, sb, ps, si, so, sc, sd, cst):
    st = so + sc
    nkt = st // 128
    with contextlib.ExitStack() as es:
        ktd = [sb(es, "ktd%d" % g, [128, st], BF16) for g in range(2)]
        va = sb(es, "vaB", [128, nkt, 258], BF16)
        qt = [sb(es, "qtB%d" % i, [128, 512], BF16) for i in range(2)]
        pt = [sb(es, "ptB%d" % i, [128, 2, 512], BF16) for i in range(3)]
        evA = sb(es, "evA", [128, 512], F32)
        evB = sb(es, "evB", [128, 512], F32)
        rden = sb(es, "rden", [128, 512], F32)
        mst = [sb(es, "mstB%d" % i, [128, 512], BF16) for i in range(2)]
        stp = [ps(es, "stB%d" % i, [128, 2, 512], F32) for i in range(3)]
        accA = ps(es, "accA", [128, 512], F32)
        accB = ps(es, "accB", [128, 512], F32)
        for g in range(2):
            for hh in range(2):
                p.dma("sp", ktd[g][hh * 64:(hh + 1) * 64, :], sd["KT"][g * 64:(g + 1) * 64, :], [sd["KT"].b], [ktd[g].b])
        vav = sd["VA"].t.rearrange("(t p) c -> p t c", p=128)
        for t0 in range(0, nkt, 8):
            t1_ = min(nkt, t0 + 8)
            p.dma("sp", va[:, t0:t1_, :], vav[:, t0:t1_, :], [sd["VA"].b], [va.b])
        p.memset("dve", evA[:], 0.0, [evA.b])
        den = TB(stp[2].t[:, 0, :])
        den.b = stp[2].b
        iters = [(qm, j) for qm in range(so // 512) for j in range(4)]

        def load_q(it):
            qm, j = iters[it]
            q = qt[it % 2]
            p.dma("sp", q[:], sd["QT"][j * 128:(j + 1) * 128, qm * 512:(qm + 1) * 512], [sd["QT"].b], [q.b])

        def qk(it, kt):
            g = iters[it][1] // 2
            q = qt[it % 2]
            s_ = stp[kt % 3]
            for hh in range(2):
                p.mm(s_[:, hh, :], ktd[g][hh * 64:(hh + 1) * 64, kt * 128:(kt + 1) * 128],
                     q[hh * 64:(hh + 1) * 64, :], True, True, [ktd[g].b, q.b], [s_.b], inc=(hh == 1))

        load_q(0)
        qk(0, 0)
        qk(0, 1)
        for it, (qm, j) in enumerate(iters):
            g = j // 2
            if it + 1 < len(iters):
                load_q(it + 1)
            for kt in range(nkt):
                if kt + 2 < nkt:
                    qk(it, kt + 2)
                s_ = stp[kt % 3]
                pp = pt[kt % 3]
                p.act(pp[:], s_[:], AF.Exp, [s_.b], [pp.b], scale=0.125)
                p.mm(accA[0:65, :], va[:, kt, g * 129 + 64:g * 129 + 129], pp[:, 0, :], kt == 0, kt == nkt - 1,
                     [va.b, pp.b], [accA.b], inc=False)
                p.mm(accB[:], va[:, kt, g * 129:g * 129 + 128], pp[:, 1, :], kt == 0, kt == nkt - 1,
                     [va.b, pp.b], [accB.b], inc=True)
            if it + 1 < len(iters):
                qk(it + 1, 0)
                qk(it + 1, 1)
            p.copy("dve", evA[0:65, :], accA[0:65, :], [accA.b], [evA.b])
            p.copy("act", evB[:], accB[:], [accB.b], [evB.b])
            p.mm(den[:], cst[0:65, C_SEL * 128:(C_SEL + 1) * 128], evA[0:65, :], True, False,
                 [cst.b, evA.b], [den.b], inc=False)
            p.mm(den[:], cst[:, C_SELB * 128:(C_SELB + 1) * 128], evB[:], False, True,
                 [cst.b, evB.b], [den.b], inc=True)
            p.copy("dve", rden[:], den[:], [den.b], [rden.b])
            p.op("dve", lambda e: e.reciprocal(out=rden[:], in_=rden[:]), [rden.b], [rden.b])
            m_ = mst[it % 2]
            p.tt("dve", m_[0:64, :], evA[0:64, :], rden[0:64, :], ALU.mult, [evA.b, rden.b], [m_.b])
            p.tt("pool", m_[64:128, :], evB[64:128, :], rden[64:128, :], ALU.mult, [evB.b, rden.b], [m_.b])
            p.dma("sp", sd["MIXT"][j * 128:(j + 1) * 128, qm * 512:(qm + 1) * 512], m_[:], [m_.b], [sd["MIXT"].b])


def phase_C(nc, p, sb, ps, si, so, sc, sd, pcol, prow, cst, cstb, epsc, aneg):
    st = so + sc
    ident = cstb[:, C_ID * 128:(C_ID + 1) * 128]
    with contextlib.ExitStack() as es:
        xw = [sb(es, "xw%d" % i, [128, 8, 516], BF16) for i in range(2)]
        cacc = [sb(es, "cacc%d" % i, [128, 512], F32) for i in range(2)]
        xc = [sb(es, "xc%d" % i, [128, 8, 512], BF16) for i in range(2)]
        xtk = [sb(es, "xtk%d" % i, [128, 4, 768], BF16) for i in range(2)]
        tpc = [ps(es, "tpc%d" % i, [128, 768], BF16) for i in range(2)]
        xbv = sd["XBC"].t.rearrange("(t p) s -> p t s", p=128)
        bcv = sd["BCT"].t.rearrange("(t p) s -> p t s", p=128)
        ci = 0
        for mt in range(st // 512):
            t0 = mt * 512
            own = t0 < so
            ntl = 8 if own else 6
            w = xw[mt % 2]
            x_ = xc[mt % 2]
            p.dma("sp", w[:], xbv[:, :, t0:t0 + 516], [sd["XBC"].b], [w.b])
            for t in range(ntl):
                a = cacc[t % 2]
                c0 = COL_CW + t * 5
                p.ts("dve", a[:], w[:, t, 0:512], pcol[:, c0:c0 + 1], None, ALU.mult, None, [w.b, pcol.b], [a.b])
                for j in range(1, 5):
                    p.stt(a[:], w[:, t, j:j + 512], pcol[:, c0 + j:c0 + j + 1], a[:], ALU.mult, ALU.add,
                          [w.b, pcol.b, a.b], [a.b])
                p.act(x_[:, t, :], a[:], AF.Silu, [a.b, pcol.b], [x_.b], bias=pcol[:, COL_CB + t:COL_CB + t + 1])
            nb = 4 if own else 2
            p.dma("sp", bcv[:, 0:nb, t0:t0 + 512], x_[:, 4:4 + nb, :], [x_.b], [sd["BCT"].b])
            xk = xtk[mt % 2]
            for c in range(4):
                tp_ = tpc[ci % 2]
                ci += 1
                for t in range(6):
                    p.tr(tp_[:, t * 128:(t + 1) * 128], x_[:, t, c * 128:(c + 1) * 128], ident, [x_.b, cstb.b], [tp_.b],
                         inc=(t == 5))
                p.copy("act", xk[:, c, :], tp_[:], [tp_.b], [xk.b])
            p.dma("sp", sd["XTOK"][t0:t0 + 512, :].rearrange("(c p) f -> p c f", p=128), xk[:], [xk.b], [sd["XTOK"].b])
    p.barrier()
    nch_own = so // 128
    nch_tot = st // 128
    G = 2
    NS = 3 * G
    for dr in (1, 0):
        chunks = list(range(nch_tot - 1, -1, -1)) if dr == 1 else list(range(nch_own))
        nchk = len(chunks)
        ctri = C_TRIDN if dr else C_TRIUP
        tri = cst[:, ctri * 128:(ctri + 1) * 128]
        ones = cst[:, C_ONES * 128:(C_ONES + 1) * 128]
        last_col = 0 if dr else 127
        with contextlib.ExitStack() as es:
            NL = 4 * G

            class _Ring:
                def __init__(self, lst, n):
                    self.lst, self.n = lst, n
                    self.cur = None

                def __getitem__(self, sl):
                    return self.lst[self.cur % self.n]

            def mk(name, shape, dt):
                return [sb(es, "%s%d" % (name, i), shape, dt) for i in range(NS)]

            def mkl(name, shape, dt):
                return _Ring([sb(es, "%s%d" % (name, i), shape, dt) for i in range(NL)], NL)
            bct = mkl("bct", [128, 4, 128], BF16)
            xtok = mkl("xtok", [128, 768], BF16)
            dtc = mkl("dtc", [128, 16], F32)
            szt = mkl("szt", [128, 512], BF16)
            ydn = mkl("ydn", [128, 512], F32)
            rings = [bct, xtok, dtc, szt, ydn]

            def setk(k_):
                for r_ in rings:
                    r_.cur = k_
            adt = mk("adt", [128, 8], F32)
            xdt = mk("xdt", [128, 512], BF16)
            rhsA = mk("rhsA", [128, 8, 128], F32)
            acs = mk("acs", [128, 16], F32)
            eA1 = rhsA
            Dm = mk("Dm", [128, 8, 128], F32)
            CBm = mk("CBm", [128, 2, 128], BF16)
            decs = mk("decs", [128, 8], F32)
            upre = mk("upre", [128, 512], F32)
            Lt = Dm
            Mt = mk("Mt", [128, 8, 128], BF16)
            Cs = mk("Cs", [128, 8, 128], BF16)
            wv = mk("wv", [128, 8], F32)
            lndt = mk("lndt", [128, 8], F32)
            xdtw = mk("xdtw", [128, 512], BF16)
            Ssb = mk("Ssb", [128, 512], F32)
            E = sb(es, "E", [128, 512], F32)
            Ebf = sb(es, "Ebf", [128, 512], BF16)
            tmpE = sb(es, "tmpE", [128, 512], F32)
            t1 = sb(es, "t1C", [128, 512], F32)
            t2 = sb(es, "t2C", [128, 512], F32)
            yn = sb(es, "ynC", [128, 512], BF16)
            junk = sb(es, "junkC", [128, 512], BF16)
            ssq = sb(es, "ssqC", [128, 2], F32)
            mxs = [sb(es, "mxs%d" % i, [128, 4, 512], BF16) for i in range(2)]
            A1 = [ps(es, "A1_%d" % i, [128, 2, 512], F32) for i in range(2)]
            cbs_ = ps(es, "cbsm", [128, 512], F32)
            cbp = [TB(cbs_.t[:, 0:256])] * 2
            smp = [TB(cbs_.t[:, 256:272]), TB(cbs_.t[:, 272:288])]
            for v_ in (cbp[0], smp[0], smp[1]):
                v_.b = cbs_.b
            yps = ps(es, "yps", [128, 512], F32)
            stp = ps(es, "stpC", [128, 512], F32)
            tpo = ps(es, "tpo", [128, 512], BF16)
            bcv = sd["BCT"].t.rearrange("(t p) s -> p t s", p=128)
            mixv = sd["MIXT"].t[512:1024, :].rearrange("(t p) s -> p t s", p=128)

            def h3(ap):
                return ap.rearrange("p (h d) -> p h d", h=8)

            import os
            _cstage = int(os.environ.get("CSTAGE", "0"))

            def loads(k0):
                ks = list(range(k0, min(k0 + G, nchk)))
                for k_ in ks:
                    setk(k_)
                    c = chunks[k_]
                    sl = k_ % NS
                    own = c < nch_own
                    r0 = c * 128
                    p.dma("sp", xtok[sl][:], sd["XTOK"][r0:r0 + 128, :], [sd["XTOK"].b], [xtok[sl].b])
                    p.dma("sp", dtc[sl][:], sd["DT"][r0:r0 + 128, :], [sd["DT"].b], [dtc[sl].b])
                    if own:
                        p.dma("sp", bct[sl][:], bcv[:, :, r0:r0 + 128], [sd["BCT"].b], [bct[sl].b])
                        if dr == 0:
                            p.dma("sp", szt[sl][:], sd["SZ"][r0:r0 + 128, :], [sd["SZ"].b], [szt[sl].b])
                            p.dma("sp", ydn[sl][:], sd["YDN"][r0:r0 + 128, :], [sd["YDN"].b], [ydn[sl].b])

            def front(k0):
                ks = list(range(k0, min(k0 + G, nchk)))
                for k_ in ks:
                    setk(k_)
                    sl = k_ % NS
                    dcol = dtc[sl][:, dr * 8:dr * 8 + 8]
                    p.tt("dve", adt[sl][:], dcol, aneg[:, dr * 8:dr * 8 + 8], ALU.mult, [dtc[sl].b, aneg.b], [adt[sl].b])
                    p.tt("dve", rhsA[sl][:], tri.unsqueeze(1).broadcast_to([128, 8, 128]),
                         adt[sl][:].unsqueeze(2).broadcast_to([128, 8, 128]), ALU.mult, [cst.b, adt[sl].b], [rhsA[sl].b])
                    p.act(lndt[sl][:], dcol, AF.Ln, [dtc[sl].b], [lndt[sl].b])
                    if dr == 0 and chunks[k_] < nch_own:
                        p.tt("pool", upre[sl][:], xtok[sl][:, 0:512], prow[:, ROW_DSKIP:ROW_DSKIP + 512], ALU.mult,
                             [xtok[sl].b, prow.b], [upre[sl].b])
                        p.tt("pool", upre[sl][:], upre[sl][:], ydn[sl][:], ALU.add, [upre[sl].b, ydn[sl].b],
                             [upre[sl].b])
                if _cstage and _cstage <= 2:
                    return
                for k_ in ks:
                    setk(k_)
                    sl = k_ % NS
                    a1 = A1[k_ % 2]
                    for a_ in range(2):
                        p.mm(a1[:, a_, :], ones, rhsA[sl][:, a_ * 4:(a_ + 1) * 4, :].rearrange("p h l -> p (h l)"),
                             True, True, [cst.b, rhsA[sl].b], [a1.b], inc=(a_ == 1))
                    sm_ = smp[k_ % 2]
                    p.mm(sm_[:, 0:8], tri, adt[sl][:], True, True, [cst.b, adt[sl].b], [sm_.b], inc=False)
                    p.mm(sm_[:, 8:16], ones, adt[sl][:], True, True, [cst.b, adt[sl].b], [sm_.b], inc=True)
                for k_ in ks:
                    setk(k_)
                    c = chunks[k_]
                    sl = k_ % NS
                    own = c < nch_own
                    a1 = A1[k_ % 2]
                    a1v = a1.t[:, :, :].rearrange("p a (h l) -> p (a h) l", l=128)
                    p.copy("dve", acs[sl][:], smp[k_ % 2][:, 0:16], [smp[k_ % 2].b], [acs[sl].b])
                    if own:
                        p.tt("dve", lndt[sl][:], acs[sl][:, 0:8], lndt[sl][:], ALU.subtract, [acs[sl].b, lndt[sl].b],
                             [lndt[sl].b])
                        p.tt("dve", Dm[sl][:], a1v, lndt[sl][:].unsqueeze(2).broadcast_to([128, 8, 128]),
                             ALU.subtract, [a1.b, lndt[sl].b], [Dm[sl].b])
                for k_ in ks:
                    setk(k_)
                    c = chunks[k_]
                    sl = k_ % NS
                    own = c < nch_own
                    a1 = A1[k_ % 2]
                    a1v = a1.t[:, :, :].rearrange("p a (h l) -> p (a h) l", l=128)
                    if own and k_ > 0:
                        p.act(Cs[sl][:], a1v, AF.Exp, [a1.b, Dm[sl].b], [Cs[sl].b])
                    p.act(decs[sl][:], acs[sl][:, 8:16], AF.Exp, [acs[sl].b], [decs[sl].b])
                if _cstage and _cstage <= 3:
                    return
                owns = [k_ for k_ in ks if chunks[k_] < nch_own]
                for k_ in owns:
                    setk(k_)
                    sl = k_ % NS
                    p.ts("dve", Dm[sl][:], Dm[sl][:], 20.0, None, ALU.min, None, [Dm[sl].b], [Dm[sl].b])
                    p.act(Mt[sl][:], Dm[sl][:], AF.Exp, [Dm[sl].b], [Mt[sl].b])
                for k_ in owns:
                    setk(k_)
                    sl = k_ % NS
                    b_ = bct[sl]
                    cb_ = cbp[k_ % 2]
                    for g in range(2):
                        p.mm(cb_[:, g * 128:(g + 1) * 128], b_[:, g, :], b_[:, 2 + g, :], True, True, [b_.b], [cb_.b],
                             inc=(g == 1))
                    p.tt("dve", CBm[sl][:], cb_[:, 0:256].rearrange("p (g l) -> p g l", g=2),
                         tri.unsqueeze(1).broadcast_to([128, 2, 128]), ALU.mult, [cb_.b, cst.b], [CBm[sl].b])
                for k_ in owns:
                    setk(k_)
                    sl = k_ % NS
                    for g in range(2):
                        p.tt("dve", Mt[sl][:, 4 * g:4 * g + 4, :], Mt[sl][:, 4 * g:4 * g + 4, :],
                             CBm[sl][:, g, :].unsqueeze(1).broadcast_to([128, 4, 128]), ALU.mult,
                             [Mt[sl].b, CBm[sl].b], [Mt[sl].b])
                for k_ in owns:
                    setk(k_)
                    sl = k_ % NS
                    b_ = bct[sl]
                    if k_ > 0:
                        for g in range(2):
                            p.tt("dve", Cs[sl][:, 4 * g:4 * g + 4, :], Cs[sl][:, 4 * g:4 * g + 4, :],
                                 b_[:, 2 + g, :].unsqueeze(1).broadcast_to([128, 4, 128]), ALU.mult,
                                 [Cs[sl].b, b_.b], [Cs[sl].b])
                if _cstage and _cstage <= 4:
                    return
                sts = [k_ for k_ in ks if k_ != nchk - 1]
                for k_ in sts:
                    setk(k_)
                    sl = k_ % NS
                    p.tt("dve", wv[sl][:], acs[sl][:, 8:16], acs[sl][:, 0:8], ALU.subtract, [acs[sl].b], [wv[sl].b])
                    p.act(wv[sl][:], wv[sl][:], AF.Exp, [wv[sl].b], [wv[sl].b])
                for k_ in sts:
                    setk(k_)
                    sl = k_ % NS
                    p.tt("dve", wv[sl][:], wv[sl][:], dtc[sl][:, dr * 8:dr * 8 + 8], ALU.mult, [wv[sl].b, dtc[sl].b],
                         [wv[sl].b])
                    p.tt("dve", h3(xdtw[sl][:]), h3(xtok[sl][:, 0:512]), wv[sl][:].unsqueeze(2).broadcast_to([128, 8, 64]),
                         ALU.mult, [xtok[sl].b, wv[sl].b], [xdtw[sl].b])
                    for g in range(2):
                        p.mm(stp[:, g * 256:(g + 1) * 256], xtok[sl][:, 512 + g * 128:512 + (g + 1) * 128],
                             xdtw[sl][:, g * 256:(g + 1) * 256], True, True, [xtok[sl].b, xdtw[sl].b], [stp.b],
                             inc=(g == 1))
                    p.copy("act", Ssb[sl][:], stp[:], [stp.b], [Ssb[sl].b])

            def back(k0):
                for k_ in range(k0, min(k0 + G, nchk)):
                    setk(k_)
                    c = chunks[k_]
                    sl = k_ % NS
                    own = c < nch_own
                    r0 = c * 128
                    hs_ = k_ > 0
                    if own:
                        for h in range(8):
                            hs = slice(h * 64, (h + 1) * 64)
                            p.mm(yps[:, hs], Mt[sl][:, h, :], xtok[sl][:, hs], True, not hs_, [Mt[sl].b, xtok[sl].b],
                                 [yps.b], inc=(h == 7 and not hs_))
                            if hs_:
                                p.mm(yps[:, hs], Cs[sl][:, h, :], Ebf[:, hs], False, True, [Cs[sl].b, Ebf.b], [yps.b],
                                     inc=(h == 7))
                    if k_ < nchk - 1:
                        if hs_:
                            p.tt("dve", h3(tmpE[:]), h3(E[:]),
                                 decs[sl][:].unsqueeze(2).broadcast_to([128, 8, 64]), ALU.mult,
                                 [E.b, decs[sl].b], [tmpE.b])
                            p.tt("dve", E[:], tmpE[:], Ssb[sl][:], ALU.add, [tmpE.b, Ssb[sl].b], [E.b])
                        else:
                            p.copy("pool", E[:], Ssb[sl][:], [Ssb[sl].b], [E.b])
                        p.copy("act", Ebf[:], E[:], [E.b], [Ebf.b])
                    if own:
                        if dr == 1:
                            p.copy("act", ydn[sl][:], yps[:], [yps.b], [ydn[sl].b])
                            p.dma("sp", sd["YDN"][r0:r0 + 128, :], ydn[sl][:], [ydn[sl].b], [sd["YDN"].b])
                        else:
                            p.tt("dve", t2[:], yps[:], upre[sl][:], ALU.add, [yps.b, upre[sl].b], [t2.b])
                            p.tt("pool", t1[:], t2[:], szt[sl][:], ALU.mult, [t2.b, szt[sl].b], [t1.b])
                            p.act(junk[:], t1[:], AF.Square, [t1.b], [junk.b, ssq.b], accum=ssq[:, 0:1])
                            p.act(ssq[:, 1:2], ssq[:, 0:1], AF.Ln, [ssq.b, epsc.b], [ssq.b], scale=1.0 / 512,
                                  bias=epsc[:, 0:1])
                            p.act(ssq[:, 1:2], ssq[:, 1:2], AF.Exp, [ssq.b], [ssq.b], scale=-0.5)
                            p.stt(yn[:], t1[:], ssq[:, 1:2], prow[:, ROW_GSSD:ROW_GSSD + 512], ALU.mult, ALU.mult,
                                  [t1.b, ssq.b, prow.b], [yn.b])
                            for t in range(4):
                                p.tr(tpo[:, t * 128:(t + 1) * 128], yn[:, t * 128:(t + 1) * 128], ident,
                                     [yn.b, cstb.b], [tpo.b], inc=(t == 3))
                            mx = mxs[(c // 4) % 2]
                            p.copy("act", mx[:, :, (c % 4) * 128:(c % 4 + 1) * 128],
                                   tpo[:].rearrange("p (t l) -> p t l", t=4), [tpo.b], [mx.b])
                            if c % 4 == 3:
                                t0 = (c // 4) * 512
                                p.dma("sp", mixv[:, :, t0:t0 + 512], mx[:], [mx.b], [sd["MIXT"].b])

            starts = list(range(0, nchk, G))
            import os
            _lim = int(os.environ.get("CLIM", "0"))
            _nob = int(os.environ.get("CNOBACK", "0"))
            if _lim:
                starts = starts[:_lim] if dr == 1 else []
            for k0 in starts[:3]:
                loads(k0)
            for k0 in starts[:2]:
                front(k0)
            for gi, k0 in enumerate(starts):
                if gi + 3 < len(starts):
                    loads(starts[gi + 3])
                if gi + 2 < len(starts):
                    front(starts[gi + 2])
                if not _nob:
                    back(k0)
        p.barrier()


def phase_D(nc, p, sb, ps, si, so, off, x_own, y_out, sd, mod_d, mc, w_out_b, w_gu_b, w_dn_b, cstb, epsc):
    with contextlib.ExitStack() as es:
        wout = sb(es, "wout", [128, 8, D], BF16)
        wdn = sb(es, "wdn", [128, 22, D], BF16)
        NWG = 4
        wgu = [sb(es, "wgu%d" % i, [128, 8, 512], BF16) for i in range(NWG)]
        g1b = sb(es, "g1b", [128, D], F32)
        g2b = sb(es, "g2b", [128, D], F32)
        mixT = sb(es, "mixT", [128, 8, 512], BF16)
        xt = [sb(es, "xD%d" % i, [128, D], F32) for i in range(4)]
        xn2 = sb(es, "xn2", [128, 4, D], BF16)
        h2T = sb(es, "h2T", [128, 8, 512], BF16)
        aT = sb(es, "aT", [128, 22, 512], BF16)
        sg = [sb(es, "sg%d" % i, [128, 512], F32) for i in range(2)]
        tmp = [sb(es, "tmpD%d" % i, [128, 512], F32) for i in range(2)]
        yo = [sb(es, "yo%d" % i, [128, D], F32) for i in range(2)]
        junk = sb(es, "junkD", [128, D], BF16)
        ssq = sb(es, "ssqD", [128, 8], F32)
        od = [ps(es, "odD%d" % i, [128, 512], F32) for i in range(2)]
        tp = [ps(es, "tpD%d" % i, [128, 512], BF16) for i in range(2)]
        gu = [ps(es, "guD%d" % i, [128, 512], F32) for i in range(4)]
        ident = cstb[:, C_ID * 128:(C_ID + 1) * 128]

        wv = w_out_b.t.rearrange("(kc p) n -> p kc n", p=128)
        for kc in range(0, 8, 4):
            p.dma("sp", wout[:, kc:kc + 4, :], wv[:, kc:kc + 4, :], [w_out_b.b], [wout.b])
        wv = w_dn_b.t.rearrange("(kc p) n -> p kc n", p=128)
        for kc in range(0, 22, 4):
            k1 = min(22, kc + 4)
            p.dma("sp", wdn[:, kc:k1, :], wv[:, kc:k1, :], [w_dn_b.b], [wdn.b])
        p.dma("sp", g1b[:], mod_d[si:si + 1, 2 * D:3 * D].broadcast_to([128, D]), [mod_d.b], [g1b.b])
        p.dma("sp", g2b[:], mod_d[si:si + 1, 5 * D:6 * D].broadcast_to([128, D]), [mod_d.b], [g2b.b])
        wguv = w_gu_b.t.rearrange("(kc p) n -> p kc n", p=128)
        mixv = sd["MIXT"].t.rearrange("(kc p) s -> p kc s", p=128)
        wi = 0
        oi = 0
        yi = 0
        for mt in range(so // 512):
            t0 = mt * 512
            p.dma("sp", mixT[:], mixv[:, :, t0:t0 + 512], [sd["MIXT"].b], [mixT.b])
            for sub in range(4):
                r0 = off + t0 + sub * 128
                p.dma("sp", xt[sub][:], x_own[r0:r0 + 128, :], (), [xt[sub].b])
            for sub in range(4):
                x_ = xt[sub]
                for half in range(2):
                    acc = od[oi % 2]
                    tm = tmp[oi % 2]
                    oi += 1
                    for kc in range(8):
                        p.mm(acc[:], mixT[:, kc, sub * 128:(sub + 1) * 128], wout[:, kc, half * 512:(half + 1) * 512],
                             kc == 0, kc == 7, [mixT.b, wout.b], [acc.b], inc=(kc == 7))
                    hs = slice(half * 512, (half + 1) * 512)
                    p.tt("dve", tm[:], acc[:], g1b[:, hs], ALU.mult, [acc.b, g1b.b], [tm.b])
                    p.tt("pool", x_[:, hs], x_[:, hs], tm[:], ALU.add, [x_.b, tm.b], [x_.b])
                p.act(junk[:], x_[:], AF.Square, [x_.b], [junk.b, ssq.b], accum=ssq[:, sub:sub + 1])
                p.act(ssq[:, 4 + sub:5 + sub], ssq[:, sub:sub + 1], AF.Sqrt, [ssq.b, epsc.b], [ssq.b],
                      scale=1.0 / D, bias=epsc[:, 0:1])
                p.op("dve", lambda e, s_=sub: e.reciprocal(out=ssq[:, 4 + s_:5 + s_], in_=ssq[:, 4 + s_:5 + s_]),
                     [ssq.b], [ssq.b])
                p.act(xn2[:, sub, :], x_[:], AF.Copy, [x_.b, ssq.b], [xn2.b], scale=ssq[:, 4 + sub:5 + sub])
            for kc in range(8):
                tpp = tp[kc % 2]
                for sub in range(4):
                    p.tr(tpp[:, sub * 128:(sub + 1) * 128], xn2[:, sub, kc * 128:(kc + 1) * 128], ident,
                         [xn2.b, cstb.b], [tpp.b], inc=(sub == 3))
                p.ts("dve", h2T[:, kc, :], tpp[:], mc[:, 2, kc:kc + 1], mc[:, 3, kc:kc + 1], ALU.mult, ALU.add,
                     [tpp.b, mc.b], [h2T.b])
            gi = 0
            for b in range(11):
                w = wgu[wi % NWG]
                wi += 1
                p.dma("sp", w[:], wguv[:, :, b * 512:(b + 1) * 512], [w_gu_b.b], [w.b])
                for t in range(2):
                    ag = gu[gi % 4]
                    au = gu[(gi + 1) % 4]
                    s_ = sg[(gi // 2) % 2]
                    gi += 2
                    for kc in range(8):
                        p.mm(ag[:], w[:, kc, t * 128:(t + 1) * 128], h2T[:, kc, :], kc == 0, kc == 7,
                             [w.b, h2T.b], [ag.b], inc=(kc == 7))
                    for kc in range(8):
                        p.mm(au[:], w[:, kc, 256 + t * 128:256 + (t + 1) * 128], h2T[:, kc, :], kc == 0, kc == 7,
                             [w.b, h2T.b], [au.b], inc=(kc == 7))
                    p.act(s_[:], ag[:], AF.Silu, [ag.b], [s_.b])
                    p.tt("dve", aT[:, 2 * b + t, :], s_[:], au[:], ALU.mult, [s_.b, au.b], [aT.b])
            for sub in range(4):
                x_ = xt[sub]
                y_ = yo[yi % 2]
                yi += 1
                for half in range(2):
                    acc = od[oi % 2]
                    tm = tmp[oi % 2]
                    oi += 1
                    for kc in range(22):
                        p.mm(acc[:], aT[:, kc, sub * 128:(sub + 1) * 128], wdn[:, kc, half * 512:(half + 1) * 512],
                             kc == 0, kc == 21, [aT.b, wdn.b], [acc.b], inc=(kc == 21))
                    hs = slice(half * 512, (half + 1) * 512)
                    p.tt("dve", tm[:], acc[:], g2b[:, hs], ALU.mult, [acc.b, g2b.b], [tm.b])
                    p.tt("pool", y_[:, hs], x_[:, hs], tm[:], ALU.add, [x_.b, tm.b], [y_.b])
                r0 = off + t0 + sub * 128
                p.dma("sp", y_out[r0:r0 + 128, :], y_[:], [y_.b], [y_out.b])


def _consts():
    c = np.zeros((128, NCONST, 128), np.float32)
    k = np.arange(128)[:, None]
    m = np.arange(128)[None, :]
    c[:, C_ID] = (k == m)
    c[:, C_BONES] = ((k // 64) == (m // 64)) / 64.0
    partner = np.where((np.arange(128) % 32) < 16, np.arange(128) + 16, np.arange(128) - 16)
    c[:, C_RPERM] = (k == partner[None, :])
    c[:, C_TRIUP] = (k <= m)
    c[:, C_TRIDN] = (k >= m)
    c[:, C_ONES] = 1.0
    c[64, C_SEL, 0:64] = 1.0
    c[0, C_SELB, 64:128] = 1.0
    return c.reshape(128, NCONST * 128)


def _rope_tab(S, rev):
    t = np.arange(S)
    if rev:
        t = t[::-1]
    rows = (t // 64).astype(np.float32)
    cols = (t % 64).astype(np.float32)
    freqs = (np.float32(10000.0) ** (-np.arange(16, dtype=np.float32) / np.float32(16))).astype(np.float32)
    tab = np.zeros((2, 64, S), np.float32)
    for d in range(64):
        pos = rows if d < 32 else cols
        ang = (pos * freqs[d % 16]).astype(np.float32)
        sign = -1.0 if (d % 32) < 16 else 1.0
        tab[0, d] = np.cos(ang)
        tab[1, d] = sign * np.sin(ang)
    return tab


def prep_core(c, I, SP, SO, NP=2):
    rev = (c % 2) == 1
    f32 = np.float32
    xs = [np.asarray(I["x_prompt"][NP * c + i]) for i in range(NP)]
    smp = np.asarray(I["x_sample"][c // 2])
    if rev:
        xs = [x[::-1] for x in xs]
        smp = smp[::-1]
    x_own = np.ascontiguousarray(np.concatenate(xs + [smp[:SO]], axis=0), dtype=f32)
    x_ctx = np.ascontiguousarray(smp[SO:], dtype=f32)
    cv = np.stack([I["c_prompt"][NP * c + i] for i in range(NP)] + [I["c_sample"][c // 2]], axis=0)
    cT = np.ascontiguousarray(cv.T.reshape(8, 128, NP + 1).transpose(1, 0, 2), dtype=f32)
    wi = np.asarray(I["w_in"][0])
    q, k, v, z, xbc = wi[:, 0:512], wi[:, 512:640], wi[:, 640:768], wi[:, 768:1280], wi[:, 1280:2304]
    dtf, dtb = wi[:, 2304:2312], wi[:, 2312:2320]
    up, dn = (dtb, dtf) if rev else (dtf, dtb)
    w_in = np.ascontiguousarray(np.concatenate([k, q, xbc, z, v, up, dn], axis=1), dtype=f32)
    wg = np.asarray(I["w_gate_up"][0])
    blocks = []
    for b in range(11):
        blocks += [wg[:, b * 256:(b + 1) * 256], wg[:, DFF + b * 256:DFF + (b + 1) * 256]]
    w_gu = np.ascontiguousarray(np.concatenate(blocks, axis=1), dtype=f32)
    pcol = np.zeros((128, NCOLP), f32)
    pcol[:, COL_QG] = np.tile(I["q_norm_g"][0], 2)
    pcol[:, COL_KG] = np.tile(I["k_norm_g"][0], 2)
    pcol[:, COL_GMIX:COL_GMIX + 8] = I["norm_mix_g"][0].reshape(8, 128).T
    pcol[:, COL_GFFN:COL_GFFN + 8] = I["norm_ffn_g"][0].reshape(8, 128).T
    pcol[:, COL_CB:COL_CB + 8] = I["conv_b"][0].reshape(8, 128).T
    cw = np.asarray(I["conv_w"][0])
    if rev:
        cw = cw[::-1]
    pcol[:, COL_CW:COL_CW + 40] = cw.reshape(5, 8, 128).transpose(2, 1, 0).reshape(128, 40)
    prow1 = np.zeros((NROWP,), f32)
    bu, bd = (I["dt_bias_bwd"][0], I["dt_bias_fwd"][0]) if rev else (I["dt_bias_fwd"][0], I["dt_bias_bwd"][0])
    au, ad = (I["a_log_bwd"][0], I["a_log_fwd"][0]) if rev else (I["a_log_fwd"][0], I["a_log_bwd"][0])
    prow1[ROW_DTB:ROW_DTB + 16] = np.concatenate([bu, bd])
    prow1[ROW_ALOG:ROW_ALOG + 16] = np.concatenate([au, ad])
    prow1[ROW_DSKIP:ROW_DSKIP + 512] = np.repeat(I["d_skip"][0], 64)
    prow1[ROW_GSSD:ROW_GSSD + 512] = I["ssd_norm_g"][0]
    prow = np.ascontiguousarray(np.broadcast_to(prow1, (128, NROWP)), dtype=f32)
    sel3 = np.zeros((NP + 1, (NP + 1) * 128), f32)
    for s in range(NP + 1):
        sel3[s, s * 128:(s + 1) * 128] = 1.0
    return {
        "x_own": x_own, "x_ctx": x_ctx, "cT": cT,
        "w_mod": np.ascontiguousarray(I["w_mod"][0], dtype=f32),
        "b_mod3": np.ascontiguousarray(np.broadcast_to(I["b_mod"][0], (NP + 1, 6 * D)), dtype=f32),
        "w_in": w_in, "w_out": np.ascontiguousarray(I["w_out"][0], dtype=f32), "w_gu": w_gu,
        "w_dn": np.ascontiguousarray(I["w_down"][0], dtype=f32),
        "pcol": pcol, "prow": prow, "consts": _consts(), "sel3": sel3,
        "rope_p": _rope_tab(SP, rev), "rope_s": _rope_tab(2 * SO, rev),
    }


def run(I, SP, SO, cores=range(8), NP=2, debug=False, phases="ABCD"):
    nc = build(SP, SO, SO, NP=NP, debug=debug, phases=phases)
    cores = list(cores)
    in_maps = [prep_core(c, I, SP, SO, NP) for c in cores]
    res = run_bass_kernel_spmd(nc, in_maps, core_ids=list(range(len(cores))))
    return res.results


def kernel(**I):
    I = {k: np.asarray(v) for k, v in I.items()}
    B, SP, _ = I["x_prompt"].shape
    DB, DS, _ = I["x_sample"].shape
    SO = DS // 2
    NP = B // 8
    results = run(I, SP, SO, range(8), NP)
    yp = np.zeros((B, SP, D), np.float32)
    ys = np.zeros((DB, DS, D), np.float32)
    for c in range(8):
        y = results[c]["y_own"]
        rev = (c % 2) == 1
        for i in range(NP):
            seg = y[i * SP:(i + 1) * SP]
            yp[NP * c + i] = seg[::-1] if rev else seg
        seg = y[NP * SP:NP * SP + SO]
        if rev:
            ys[c // 2, SO:] = seg[::-1]
        else:
            ys[c // 2, :SO] = seg
    return (yp, ys)
```

```python
import contextlib
import numpy as np
import concourse.bass as bass
import concourse.mybir as mybir
from concourse.bass_utils import run_bass_kernel_spmd

F32 = mybir.dt.float32
BF16 = mybir.dt.bfloat16
AF = mybir.ActivationFunctionType
ALU = mybir.AluOpType

D = 1024
NH = 8
EPS = 1e-6
NDS = 16
NHW = 12
import itertools
_uid = itertools.count()


class Buf:
    __slots__ = ("w", "r", "acc")

    def __init__(self, acc=False):
        self.w = {}
        self.r = {}
        self.acc = acc


class Prog:
    def __init__(self, nc, es):
        self.nc = nc
        self.eng = {"pe": nc.tensor, "act": nc.scalar, "dve": nc.vector, "pool": nc.gpsimd, "sp": nc.sync}
        self.semobj = {}
        for k in ["pe", "act", "dve", "pool"]:
            self.semobj[k] = es.enter_context(nc.semaphore("sem_" + k))
        for i in range(NDS):
            self.semobj[("d", i)] = es.enter_context(nc.semaphore("dsem%d" % i))
        self.cnt = {k: 0 for k in ["pe", "act", "dve", "pool"]}
        self.seen = {k: {} for k in self.eng}
        self.dval = [0] * NDS
        self.dnext = 0
        self.dnext_sw = 0
        self.nwait = 0
        self.nins = 0
        self.log = []

    def _wait(self, ek, sk, v):
        if self.seen[ek].get(sk, 0) >= v:
            return
        if sk == ek and ek == "pe":
            return
        self.eng[ek].wait_ge(self.semobj[sk], v)
        self.log.append((ek, 'WAIT', sk, v))
        self.seen[ek][sk] = v
        self.nwait += 1

    def _deps(self, ek, reads, writes):
        for b in reads:
            for sk, v in b.w.items():
                self._wait(ek, sk, v)
        for b in writes:
            if not b.acc:
                for sk, v in b.w.items():
                    self._wait(ek, sk, v)
            for sk, v in b.r.items():
                self._wait(ek, sk, v)

    def _mark(self, tok, reads, writes):
        sk, v = tok
        for b in reads:
            if b.r.get(sk, 0) < v:
                b.r[sk] = v
        for b in writes:
            if b.acc:
                if b.w.get(sk, 0) < v:
                    b.w[sk] = v
            else:
                b.w = {sk: v}
                b.r = {}

    def op(self, ek, fn, reads=(), writes=(), inc=True):
        self._deps(ek, reads, writes)
        ins = fn(self.eng[ek])
        self.nins += 1
        tok = (ek, self.cnt[ek] + 1)
        self.log.append((ek, 'OP', self.cnt[ek] + 1, inc))
        if inc:
            ins.then_inc(self.semobj[ek], 1)
            self.cnt[ek] += 1
        self._mark(tok, reads, writes)

    def dma(self, qk, out, in_, reads=(), writes=(), **kw):
        self._deps(qk, reads, writes)
        if qk == "pool":
            i = NHW + self.dnext_sw
            self.dnext_sw = (self.dnext_sw + 1) % (NDS - NHW)
        else:
            i = self.dnext
            self.dnext = (i + 1) % NHW
        if self.dval[i] > 0:
            self._wait(qk, ("d", i), self.dval[i])
        self.dval[i] += 16
        self.eng[qk].dma_start(out=out, in_=in_, **kw).then_inc(self.semobj[("d", i)], 16)
        self.nins += 1
        self.log.append((qk, 'DMA', i, self.dval[i]))
        self._mark((("d", i), self.dval[i]), reads, writes)

    def barrier(self):
        self.log.append(('*', 'BARRIER', 0, 0))
        for ek in ["pe", "act", "dve", "pool", "sp"]:
            for i in range(NDS):
                if self.dval[i] > 0:
                    self._wait(ek, ("d", i), self.dval[i])
            for k in ["pe", "act", "dve", "pool"]:
                if k != ek and self.cnt[k] > 0:
                    self._wait(ek, k, self.cnt[k])

    def finish(self, ek="sp"):
        for i in range(NDS):
            if self.dval[i] > 0:
                self._wait(ek, ("d", i), self.dval[i])
        for k in ["pe", "act", "dve", "pool"]:
            if self.cnt[k] > 0:
                self._wait(ek, k, self.cnt[k])

    def mm(self, out, lhsT, rhs, start, stop, R, W, inc=True):
        self.op("pe", lambda e: e.matmul(out, lhsT=lhsT, rhs=rhs, start=start, stop=stop), R, W, inc)

    def tr(self, out, in_, ident, R, W, inc=True):
        self.op("pe", lambda e: e.transpose(out, in_, ident), R, W, inc)

    def act(self, out, in_, func, R, W, scale=None, bias=None, accum=None):
        kw = {}
        if scale is not None:
            kw["scale"] = scale
        if bias is not None:
            kw["bias"] = bias
        if accum is not None:
            kw["accum_out"] = accum
        self.op("act", lambda e: e.activation(out=out, in_=in_, func=func, **kw), R, W)

    def tt(self, ek, out, in0, in1, op, R, W):
        self.op(ek, lambda e: e.tensor_tensor(out=out, in0=in0, in1=in1, op=op), R, W)

    def ts(self, ek, out, in0, s1, s2, op0, op1, R, W):
        if op1 is None:
            self.op(ek, lambda e: e.tensor_scalar(out=out, in0=in0, scalar1=s1, scalar2=None, op0=op0), R, W)
        else:
            self.op(ek, lambda e: e.tensor_scalar(out=out, in0=in0, scalar1=s1, scalar2=s2, op0=op0, op1=op1), R, W)

    def stt(self, out, in0, scalar, in1, op0, op1, R, W):
        self.op("dve", lambda e: e.scalar_tensor_tensor(out=out, in0=in0, scalar=scalar, in1=in1, op0=op0, op1=op1), R, W)

    def copy(self, ek, out, in_, R, W):
        if ek == "act":
            self.op(ek, lambda e: e.copy(out=out, in_=in_), R, W)
        else:
            self.op(ek, lambda e: e.tensor_copy(out=out, in_=in_), R, W)

    def memset(self, ek, ap, val, W):
        self.op(ek, lambda e: e.memset(ap, val), (), W)


class TB:
    def __init__(self, t, acc=False):
        self.t = t
        self.b = Buf(acc)

    def __getitem__(self, idx):
        return self.t[idx]


COL_QG, COL_KG, COL_GMIX, COL_GFFN, COL_CB, COL_CW, NCOLP = 0, 1, 2, 10, 18, 26, 66
ROW_DTB, ROW_ALOG, ROW_DSKIP, ROW_GSSD, NROWP = 0, 16, 32, 544, 1056
C_ID, C_BONES, C_RPERM, C_TRIUP, C_TRIDN, C_ONES, C_SEL, C_SELB, NCONST = 0, 1, 2, 3, 4, 5, 6, 7, 8
NFM = 13
TM0 = NFM * 128
NIN = 2320
DFF = 2816


def build(SP, SO, SC, NP=2, debug=False, phases="ABCD"):
    nc = bass.Bass("TRN2", target_bir_lowering=False)
    segs = [(SP, 0)] * NP + [(SO, SC)]
    NSEG = len(segs)
    offs = [sum(s[0] for s in segs[:i]) for i in range(NSEG)]
    TOT = sum(s[0] for s in segs)
    SKMAX = max(s[0] + s[1] for s in segs)

    def din(name, shape, dt=F32):
        return nc.dram_tensor(name, list(shape), dt, kind="ExternalInput").ap()

    def dscr(name, shape, dt):
        kind = "ExternalOutput" if debug else "Internal"
        return TB(nc.dram_tensor(name, list(shape), dt, kind=kind).ap(), acc=True)

    x_own = din("x_own", [TOT, D])
    x_ctx = din("x_ctx", [max(SC, 128), D])
    cT_d = din("cT", [128, 8, NSEG])
    w_mod = din("w_mod", [D, 6 * D])
    b_mod3 = din("b_mod3", [NSEG, 6 * D])
    w_in = din("w_in", [D, NIN])
    w_out = din("w_out", [D, D])
    w_gu = din("w_gu", [D, 2 * DFF])
    w_dn = din("w_dn", [DFF, D])
    pcol_d = din("pcol", [128, NCOLP])
    prow_d = din("prow", [128, NROWP])
    const_d = din("consts", [128, NCONST * 128])
    sel3_d = din("sel3", [NSEG, NSEG * 128])
    rope_p = din("rope_p", [2, 64, SP])
    rope_s = din("rope_s", [2, 64, SO + SC])
    y_out = TB(nc.dram_tensor("y_own", [TOT, D], F32, kind="ExternalOutput").ap(), acc=True)

    w_in_b = dscr("w_in_b", [D, NIN], BF16)
    w_out_b = dscr("w_out_b", [D, D], BF16)
    w_gu_b = dscr("w_gu_b", [D, 2 * DFF], BF16)
    w_dn_b = dscr("w_dn_b", [DFF, D], BF16)
    mod_d = dscr("mod_d", [NSEG, 6 * D], F32)
    S = []
    for si, (so, sc) in enumerate(segs):
        st = so + sc
        S.append(dict(
            QT=dscr("QT%d" % si, [512, so], BF16),
            KT=dscr("KT%d" % si, [128, st], BF16),
            VA=dscr("VA%d" % si, [st, 258], BF16),
            XBC=dscr("XBC%d" % si, [1024, st + 4], BF16),
            SZ=dscr("SZ%d" % si, [so, 512], BF16),
            DT=dscr("DT%d" % si, [st, 16], F32),
            MIXT=dscr("MIXT%d" % si, [1024, so], BF16),
            BCT=dscr("BCT%d" % si, [512, st], BF16),
            XTOK=dscr("XTOK%d" % si, [st, 768], BF16),
            YDN=dscr("YDN%d" % si, [so, 512], F32),
        ))

    top = contextlib.ExitStack()
    with top:
        p = Prog(nc, top)

        def sb(es, name, shape, dt):
            return TB(es.enter_context(nc.sbuf_tensor("s%d_%s" % (next(_uid), name), list(shape), dt)))

        def ps(es, name, shape, dt):
            return TB(es.enter_context(nc.psum_tensor("p%d_%s" % (next(_uid), name), list(shape), dt)))

        cst = sb(top, "cst", [128, NCONST * 128], F32)
        cstb = sb(top, "cstb", [128, NCONST * 128], BF16)
        pcol = sb(top, "pcol", [128, NCOLP], F32)
        prow = sb(top, "prow", [128, NROWP], F32)
        epsc = sb(top, "epsc", [128, 2], F32)
        p.dma("sp", cst[:], const_d, (), [cst.b])
        p.dma("sp", pcol[:], pcol_d, (), [pcol.b])
        p.dma("sp", prow[:], prow_d, (), [prow.b])
        p.copy("dve", cstb[:], cst[:], [cst.b], [cstb.b])
        p.memset("dve", epsc[:, 0:1], EPS, [epsc.b])
        p.memset("dve", epsc[:, 1:2], 1.0, [epsc.b])
        aneg = sb(top, "aneg", [128, 16], F32)
        p.act(aneg[:], prow[:, ROW_ALOG:ROW_ALOG + 16], AF.Exp, [prow.b], [aneg.b])
        p.ts("dve", aneg[:], aneg[:], -1.0, None, ALU.mult, None, [aneg.b], [aneg.b])

        def cblk(i, bf=False):
            return (cstb if bf else cst)[:, i * 128:(i + 1) * 128]

        for (src, dst, rows) in [(w_in, w_in_b, D), (w_out, w_out_b, D), (w_gu, w_gu_b, D), (w_dn, w_dn_b, DFF)]:
            for r0 in range(0, rows, 128):
                p.dma("pool", dst[r0:r0 + 128, :], src[r0:r0 + 128, :], (), [dst.b])

        with contextlib.ExitStack() as es:
            cT = sb(es, "cTs", [128, 8, NSEG], F32)
            scT = sb(es, "scT", [128, 8, NSEG], F32)
            bm = sb(es, "bm", [NSEG, 6 * D], F32)
            wm = [sb(es, "wm%d" % i, [128, 8, 512], F32) for i in range(2)]
            mst = [sb(es, "mst%d" % i, [NSEG, 512], F32) for i in range(2)]
            mps = [ps(es, "mps%d" % i, [NSEG, 512], F32) for i in range(2)]
            p.dma("sp", cT[:], cT_d, (), [cT.b])
            p.dma("sp", bm[:], b_mod3, (), [bm.b])
            p.act(scT[:], cT[:], AF.Silu, [cT.b], [scT.b])
            wmv = w_mod.rearrange("(kc p) n -> p kc n", p=128)
            for nt in range(12):
                w = wm[nt % 2]
                p.dma("sp", w[:], wmv[:, :, nt * 512:(nt + 1) * 512], (), [w.b])
                m = mps[nt % 2]
                for kc in range(8):
                    p.mm(m[:], scT[:, kc, :], w[:, kc, :], kc == 0, kc == 7, [scT.b, w.b], [m.b], inc=(kc == 7))
                s_ = mst[nt % 2]
                p.tt("dve", s_[:], m[:], bm[:, nt * 512:(nt + 1) * 512], ALU.add, [m.b, bm.b], [s_.b])
                p.dma("sp", mod_d[:, nt * 512:(nt + 1) * 512], s_[:], [s_.b], [mod_d.b])
            p.barrier()

        modc = [sb(top, "modc%d" % si, [128, 4, 8], F32) for si in range(NSEG)]
        with contextlib.ExitStack() as es:
            tmpc = sb(es, "tmpc", [128, 4, 8], F32)
            for si in range(NSEG):
                mc = modc[si]
                for qi, q in enumerate([1, 0, 4, 3]):
                    src = mod_d[si, q * D:(q + 1) * D].rearrange("(kc p) -> p kc", p=128)
                    p.dma("sp", tmpc[:, qi, :], src, [mod_d.b], [tmpc.b], allow_slow_non_contiguous=True)
                for (qi, gcol, oi) in [(0, COL_GMIX, 0), (2, COL_GFFN, 2)]:
                    p.ts("dve", mc[:, oi, :], tmpc[:, qi, :], 1.0, None, ALU.add, None, [tmpc.b], [mc.b])
                    p.tt("dve", mc[:, oi, :], mc[:, oi, :], pcol[:, gcol:gcol + 8], ALU.mult, [mc.b, pcol.b], [mc.b])
                p.copy("dve", mc[:, 1, :], tmpc[:, 1, :], [tmpc.b], [mc.b])
                p.copy("dve", mc[:, 3, :], tmpc[:, 3, :], [tmpc.b], [mc.b])
            p.barrier()

        for si, (so, sc) in enumerate(segs):
            st = so + sc
            sd = S[si]
            rope = rope_p if si < NP else rope_s
            p.barrier()
            if "A" in phases:
                phase_A(nc, p, sb, ps, si, so, sc, offs[si], x_own, x_ctx, w_in_b, sd, rope, modc[si],
                        pcol, prow, cst, cstb, epsc)
            p.barrier()
            if "B" in phases:
                phase_B(nc, p, sb, ps, si, so, sc, sd, cst)
            p.barrier()
            if "C" not in phases and "D" in phases:
                with contextlib.ExitStack() as es:
                    zt = sb(es, "zt", [128, 512], BF16)
                    p.memset("dve", zt[:], 0.0, [zt.b])
                    for r0 in range(512, 1024, 128):
                        for c0 in range(0, so, 512):
                            p.dma("sp", sd["MIXT"][r0:r0 + 128, c0:c0 + 512], zt[:], [zt.b], [sd["MIXT"].b])
                p.barrier()
            if "C" in phases:
                phase_C(nc, p, sb, ps, si, so, sc, sd, pcol, prow, cst, cstb, epsc, aneg)
            p.barrier()
            if "D" in phases:
                phase_D(nc, p, sb, ps, si, so, offs[si], x_own, y_out, sd, mod_d, modc[si], w_out_b, w_gu_b,
                        w_dn_b, cstb, epsc)
        p.finish("sp")
    build.last_log = p.log
    print("instructions", p.nins, "waits", p.nwait, "counts", p.cnt)
    return nc


def phase_A(nc, p, sb, ps, si, so, sc, off, x_own, x_ctx, w_in_b, sd, rope, mc, pcol, prow, cst, cstb, epsc):
    st = so + sc
    with contextlib.ExitStack() as es:
        win = sb(es, "win", [128, 8, NIN], BF16)
        wv = w_in_b.t.rearrange("(kc p) n -> p kc n", p=128)
        for kc in range(8):
            p.dma("sp", win[:, kc, :], wv[:, kc, :], [w_in_b.b], [win.b])
        xs = [sb(es, "xs%d" % i, [128, D], F32) for i in range(3)]
        junk = sb(es, "junk", [128, D], BF16)
        ssq = sb(es, "ssq", [128, 8], F32)
        xn = [sb(es, "xn%d" % i, [128, 4, D], BF16) for i in range(2)]
        hT = [sb(es, "hT%d" % i, [128, 8, 512], BF16) for i in range(2)]
        sq = [sb(es, "sq%d" % i, [128, 512], F32) for i in range(2)]
        qg = [sb(es, "qg%d" % i, [128, 512], BF16) for i in range(2)]
        lnv = [sb(es, "lnv%d" % i, [128, 512], F32) for i in range(2)]
        t1 = [sb(es, "t1%d" % i, [128, 512], F32) for i in range(2)]
        t2 = [sb(es, "t2%d" % i, [128, 512], F32) for i in range(2)]
        qo = [sb(es, "qo%d" % i, [128, 512], BF16) for i in range(3)]
        ctab = [sb(es, "ctab%d" % i, [128, 2, 512], F32) for i in range(2)]
        xbs = [sb(es, "xbs%d" % i, [128, 8, 512], BF16) for i in range(2)]
        szs = [sb(es, "szs%d" % i, [128, 4, 512], BF16) for i in range(2)]
        vas = [sb(es, "vas%d" % i, [128, 4, 2, 129], BF16) for i in range(2)]
        dts = [sb(es, "dts%d" % i, [128, 4, 16], F32) for i in range(2)]
        dtr = sb(es, "dtr", [128, 4, 16], F32)
        zpad = sb(es, "zpad", [128, 8, 2], BF16)
        tp = [ps(es, "tpA%d" % i, [128, 512], BF16) for i in range(2)]
        fm = [ps(es, "fmA%d" % i, [128, 512], F32) for i in range(2)]
        tmz = [ps(es, "tmz", [128, 512], F32)] * 2
        tmv = [ps(es, "tmv", [128, 256], F32)] * 2
        ssb = ps(es, "ssbA", [128, 512], F32)
        rot = ps(es, "rotA", [128, 512], F32)

        for v_ in vas:
            p.memset("pool", v_[:], 0.0, [v_.b])
            p.memset("pool", v_[:, :, :, 0:1], 1.0, [v_.b])
            p.memset("pool", v_[:, :, :, 128:129], 1.0, [v_.b])
        p.memset("pool", zpad[:], 0.0, [zpad.b])
        xbv = sd["XBC"].t.rearrange("(t p) s -> p t s", p=128)
        p.dma("sp", xbv[:, :, 0:2], zpad[:], [zpad.b], [sd["XBC"].b])
        p.dma("sp", xbv[:, :, st + 2:st + 4], zpad[:], [zpad.b], [sd["XBC"].b])

        ident = cstb[:, C_ID * 128:(C_ID + 1) * 128]
        nmt = st // 512
        qi = 0
        xi = 0
        def part1(mt):
            own = (mt * 512) < so
            t0 = mt * 512
            par = mt % 2
            nonlocal xi
            ct = ctab[par]
            for hh in range(2):
                p.dma("sp", ct[hh * 64:(hh + 1) * 64, :, :],
                      rope[:, :, t0:t0 + 512].rearrange("c d s -> d c s"), (), [ct.b])
            xnt = xn[par]
            for sub in range(4):
                xt = xs[xi % 3]
                xi += 1
                if own:
                    src = x_own[off + t0 + sub * 128: off + t0 + (sub + 1) * 128, :]
                else:
                    src = x_ctx[t0 - so + sub * 128: t0 - so + (sub + 1) * 128, :]
                p.dma("sp", xt[:], src, (), [xt.b])
                p.act(junk[:], xt[:], AF.Square, [xt.b], [junk.b, ssq.b], accum=ssq[:, sub:sub + 1])
                p.act(ssq[:, 4 + sub:5 + sub], ssq[:, sub:sub + 1], AF.Ln, [ssq.b, epsc.b], [ssq.b],
                      scale=1.0 / D, bias=epsc[:, 0:1])
                p.act(ssq[:, 4 + sub:5 + sub], ssq[:, 4 + sub:5 + sub], AF.Exp, [ssq.b], [ssq.b], scale=-0.5)
                p.act(xnt[:, sub, :], xt[:], AF.Copy, [xt.b, ssq.b], [xnt.b], scale=ssq[:, 4 + sub:5 + sub])
            h = hT[par]
            for kc in range(8):
                tpp = tp[kc % 2]
                for sub in range(4):
                    p.tr(tpp[:, sub * 128:(sub + 1) * 128], xnt[:, sub, kc * 128:(kc + 1) * 128], ident,
                         [xnt.b, cstb.b], [tpp.b], inc=(sub == 3))
                p.ts("dve", h[:, kc, :], tpp[:], mc[:, 0, kc:kc + 1], mc[:, 1, kc:kc + 1], ALU.mult, ALU.add,
                     [tpp.b, mc.b], [h.b])

        def part2(mt):
            own = (mt * 512) < so
            t0 = mt * 512
            par = mt % 2
            nonlocal qi
            ct = ctab[par]
            h = hT[par]
            xb = xbs[par]
            ftiles = list(range(NFM)) if own else [0] + list(range(5, NFM))
            pending = []

            def post_qk(f, j, t0=t0, ct=ct):
                p.mm(ssb[:], cst[:, C_BONES * 128:(C_BONES + 1) * 128], sq[j][:], True, True,
                     [cst.b, sq[j].b], [ssb.b])
                p.mm(rot[:], cstb[:, C_RPERM * 128:(C_RPERM + 1) * 128], qg[j][:], True, True,
                     [cstb.b, qg[j].b], [rot.b])
                p.act(lnv[j][:], ssb[:], AF.Ln, [ssb.b, epsc.b], [lnv[j].b], bias=epsc[:, 0:1])
                p.act(lnv[j][:], lnv[j][:], AF.Exp, [lnv[j].b], [lnv[j].b], scale=-0.5)
                p.tt("dve", t1[j][:], qg[j][:], ct[:, 0, :], ALU.mult, [qg[j].b, ct.b], [t1[j].b])
                p.tt("dve", t2[j][:], rot[:], ct[:, 1, :], ALU.mult, [rot.b, ct.b], [t2[j].b])
                p.tt("pool", t1[j][:], t1[j][:], t2[j][:], ALU.add, [t1[j].b, t2[j].b], [t1[j].b])
                o = qo[f % 3]
                p.tt("pool", o[:], t1[j][:], lnv[j][:], ALU.mult, [t1[j].b, lnv[j].b], [o.b])
                if f == 0:
                    p.dma("sp", sd["KT"][:, t0:t0 + 512], o[:], [o.b], [sd["KT"].b])
                else:
                    p.dma("sp", sd["QT"][(f - 1) * 128:f * 128, t0:t0 + 512], o[:], [o.b], [sd["QT"].b])

            for f in ftiles:
                acc = fm[qi % 2]
                qi += 1
                for kc in range(8):
                    p.mm(acc[:], win[:, kc, f * 128:(f + 1) * 128], h[:, kc, :], kc == 0, kc == 7,
                         [win.b, h.b], [acc.b], inc=(kc == 7))
                for (pf, pj) in pending:
                    post_qk(pf, pj)
                pending = []
                if f < 5:
                    j = f % 2
                    gcol = COL_KG if f == 0 else COL_QG
                    p.act(sq[j][:], acc[:], AF.Square, [acc.b], [sq[j].b])
                    p.act(qg[j][:], acc[:], AF.Copy, [acc.b, pcol.b], [qg[j].b], scale=pcol[:, gcol:gcol + 1])
                    pending.append((f, j))
                else:
                    eng = "act" if (f % 2) else "dve"
                    p.copy(eng, xb[:, f - 5, :], acc[:], [acc.b], [xb.b])
            for (pf, pj) in pending:
                post_qk(pf, pj)
            ntl = 8
            p.dma("sp", xbv[:, 0:ntl, 2 + t0:2 + t0 + 512], xb[:, 0:ntl, :], [xb.b], [sd["XBC"].b])
            sz = szs[par]
            va = vas[par]
            dtt = dts[par]
            for sub in range(4):
                if own:
                    az = tmz[sub % 2]
                    for kc in range(8):
                        p.mm(az[:], h[:, kc, sub * 128:(sub + 1) * 128], win[:, kc, TM0:TM0 + 512], kc == 0, kc == 7,
                             [win.b, h.b], [az.b], inc=(kc == 7))
                    p.act(sz[:, sub, :], az[:], AF.Silu, [az.b], [sz.b])
                av = tmv[sub % 2]
                for kc in range(8):
                    p.mm(av[:, 0:144], h[:, kc, sub * 128:(sub + 1) * 128], win[:, kc, TM0 + 512:TM0 + 656], kc == 0,
                         kc == 7, [win.b, h.b], [av.b], inc=(kc == 7))
                p.copy("dve", va[:, sub, :, 64:128], av[:, 0:128].rearrange("p (g d) -> p g d", g=2), [av.b], [va.b])
                p.tt("dve", dtr[:, sub, :], av[:, 128:144], prow[:, ROW_DTB:ROW_DTB + 16], ALU.add, [av.b, prow.b],
                     [dtr.b])
            p.act(dtr[:], dtr[:], AF.Exp, [dtr.b], [dtr.b])
            p.act(dtt[:], dtr[:], AF.Ln, [dtr.b, epsc.b], [dtt.b], bias=epsc[:, 1:2])
            if own:
                p.dma("sp", sd["SZ"][t0:t0 + 512, :].rearrange("(s p) c -> p s c", p=128), sz[:], [sz.b], [sd["SZ"].b])
            p.dma("sp", sd["VA"][t0:t0 + 512, :].rearrange("(s p) c -> p s c", p=128),
                  va[:].rearrange("p s g c -> p s (g c)"), [va.b], [sd["VA"].b])
            p.dma("sp", sd["DT"][t0:t0 + 512, :].rearrange("(s p) c -> p s c", p=128), dtt[:], [dtt.b], [sd["DT"].b])

        part1(0)
        for mt in range(nmt):
            if mt + 1 < nmt:
                part1(mt + 1)
            part2(mt)


def phase_B(nc, p, sb, ps, si, so, sc, sd, cst):
    st = so + sc
    nkt = st // 128
    with contextlib.ExitStack() as es:
        ktd = [sb(es, "ktd%d" % g, [128, st], BF16) for g in range(2)]
        va = sb(es, "vaB", [128, nkt, 258], BF16)
        qt = [sb(es, "qtB%d" % i, [128, 512], BF16) for i in range(2)]
        pt = [sb(es, "ptB%d" % i, [128, 2, 512], BF16) for i in range(4)]
        evA = sb(es, "evA", [128, 512], F32)
        evB = sb(es, "evB", [128, 512], F32)
        rden = sb(es, "rden", [128, 512], F32)
        mst = [sb(es, "mstB%d" % i, [128, 512], BF16) for i in range(2)]
        stp = [ps(es, "stB%d" % i, [128, 2, 512], F32) for i in range(3)]
        accA = ps(es, "accA", [128, 512], F32)
        accB = ps(es, "accB", [128, 512], F32)
        for g in range(2):
            for hh in range(2):
                p.dma("sp", ktd[g][hh * 64:(hh + 1) * 64, :], sd["KT"][g * 64:(g + 1) * 64, :], [sd["KT"].b], [ktd[g].b])
        vav = sd["VA"].t.rearrange("(t p) c -> p t c", p=128)
        for t0 in range(0, nkt, 8):
            t1_ = min(nkt, t0 + 8)
            p.dma("sp", va[:, t0:t1_, :], vav[:, t0:t1_, :], [sd["VA"].b], [va.b])
        p.memset("dve", evA[:], 0.0, [evA.b])
        den = TB(stp[2].t[:, 0, :])
        den.b = stp[2].b
        iters = [(qm, j) for qm in range(so // 512) for j in range(4)]

        def load_q(it):
            qm, j = iters[it]
            q = qt[it % 2]
            p.dma("sp", q[:], sd["QT"][j * 128:(j + 1) * 128, qm * 512:(qm + 1) * 512], [sd["QT"].b], [q.b])

        def qk(it, kt):
            g = iters[it][1] // 2
            q = qt[it % 2]
            s_ = stp[kt % 3]
            for hh in range(2):
                p.mm(s_[:, hh, :], ktd[g][hh * 64:(hh + 1) * 64, kt * 128:(kt + 1) * 128],
                     q[hh * 64:(hh + 1) * 64, :], True, True, [ktd[g].b, q.b], [s_.b], inc=(hh == 1))

        load_q(0)
        qk(0, 0)
        qk(0, 1)
        for it, (qm, j) in enumerate(iters):
            g = j // 2
            if it + 1 < len(iters):
                load_q(it + 1)
            for kt in range(nkt):
                if kt + 2 < nkt:
                    qk(it, kt + 2)
                s_ = stp[kt % 3]
                pp = pt[kt % 4]
                p.act(pp[:], s_[:], AF.Exp, [s_.b], [pp.b], scale=0.125)
                p.mm(accA[0:65, :], va[:, kt, g * 129 + 64:g * 129 + 129], pp[:, 0, :], kt == 0, kt == nkt - 1,
                     [va.b, pp.b], [accA.b], inc=False)
                p.mm(accB[:], va[:, kt, g * 129:g * 129 + 128], pp[:, 1, :], kt == 0, kt == nkt - 1,
                     [va.b, pp.b], [accB.b], inc=True)
            if it + 1 < len(iters):
                qk(it + 1, 0)
                qk(it + 1, 1)
            p.copy("dve", evA[0:65, :], accA[0:65, :], [accA.b], [evA.b])
            p.copy("act", evB[:], accB[:], [accB.b], [evB.b])
            p.mm(den[:], cst[0:65, C_SEL * 128:(C_SEL + 1) * 128], evA[0:65, :], True, False,
                 [cst.b, evA.b], [den.b], inc=False)
            p.mm(den[:], cst[:, C_SELB * 128:(C_SELB + 1) * 128], evB[:], False, True,
                 [cst.b, evB.b], [den.b], inc=True)
            p.copy("dve", rden[:], den[:], [den.b], [rden.b])
            p.op("dve", lambda e: e.reciprocal(out=rden[:], in_=rden[:]), [rden.b], [rden.b])
            m_ = mst[it % 2]
            p.tt("dve", m_[0:64, :], evA[0:64, :], rden[0:64, :], ALU.mult, [evA.b, rden.b], [m_.b])
            p.tt("pool", m_[64:128, :], evB[64:128, :], rden[64:128, :], ALU.mult, [evB.b, rden.b], [m_.b])
            p.dma("sp", sd["MIXT"][j * 128:(j + 1) * 128, qm * 512:(qm + 1) * 512], m_[:], [m_.b], [sd["MIXT"].b])


def phase_C(nc, p, sb, ps, si, so, sc, sd, pcol, prow, cst, cstb, epsc, aneg):
    st = so + sc
    ident = cstb[:, C_ID * 128:(C_ID + 1) * 128]
    with contextlib.ExitStack() as es:
        xw = [sb(es, "xw%d" % i, [128, 8, 516], BF16) for i in range(2)]
        cacc = [sb(es, "cacc%d" % i, [128, 512], F32) for i in range(2)]
        xc = [sb(es, "xc%d" % i, [128, 8, 512], BF16) for i in range(2)]
        xtk = [sb(es, "xtk%d" % i, [128, 4, 768], BF16) for i in range(2)]
        tpc = [ps(es, "tpc%d" % i, [128, 768], BF16) for i in range(2)]
        xbv = sd["XBC"].t.rearrange("(t p) s -> p t s", p=128)
        bcv = sd["BCT"].t.rearrange("(t p) s -> p t s", p=128)
        ci = 0
        for mt in range(st // 512):
            t0 = mt * 512
            own = t0 < so
            ntl = 8 if own else 6
            w = xw[mt % 2]
            x_ = xc[mt % 2]
            p.dma("sp", w[:], xbv[:, :, t0:t0 + 516], [sd["XBC"].b], [w.b])
            for t in range(ntl):
                a = cacc[t % 2]
                c0 = COL_CW + t * 5
                p.ts("dve", a[:], w[:, t, 0:512], pcol[:, c0:c0 + 1], None, ALU.mult, None, [w.b, pcol.b], [a.b])
                for j in range(1, 5):
                    p.stt(a[:], w[:, t, j:j + 512], pcol[:, c0 + j:c0 + j + 1], a[:], ALU.mult, ALU.add,
                          [w.b, pcol.b, a.b], [a.b])
                p.act(x_[:, t, :], a[:], AF.Silu, [a.b, pcol.b], [x_.b], bias=pcol[:, COL_CB + t:COL_CB + t + 1])
            nb = 4 if own else 2
            p.dma("sp", bcv[:, 0:nb, t0:t0 + 512], x_[:, 4:4 + nb, :], [x_.b], [sd["BCT"].b])
            xk = xtk[mt % 2]
            for c in range(4):
                tp_ = tpc[ci % 2]
                ci += 1
                for t in range(6):
                    p.tr(tp_[:, t * 128:(t + 1) * 128], x_[:, t, c * 128:(c + 1) * 128], ident, [x_.b, cstb.b], [tp_.b],
                         inc=(t == 5))
                p.copy("act", xk[:, c, :], tp_[:], [tp_.b], [xk.b])
            p.dma("sp", sd["XTOK"][t0:t0 + 512, :].rearrange("(c p) f -> p c f", p=128), xk[:], [xk.b], [sd["XTOK"].b])
    p.barrier()
    nch_own = so // 128
    nch_tot = st // 128
    G = 2
    NS = 3 * G
    for dr in (1, 0):
        chunks = list(range(nch_tot - 1, -1, -1)) if dr == 1 else list(range(nch_own))
        nchk = len(chunks)
        ctri = C_TRIDN if dr else C_TRIUP
        tri = cst[:, ctri * 128:(ctri + 1) * 128]
        ones = cst[:, C_ONES * 128:(C_ONES + 1) * 128]
        last_col = 0 if dr else 127
        with contextlib.ExitStack() as es:
            NL = 4 * G

            class _Ring:
                def __init__(self, lst, n):
                    self.lst, self.n = lst, n
                    self.cur = None

                def __getitem__(self, sl):
                    return self.lst[self.cur % self.n]

            def mk(name, shape, dt):
                return [sb(es, "%s%d" % (name, i), shape, dt) for i in range(NS)]

            def mkl(name, shape, dt):
                return _Ring([sb(es, "%s%d" % (name, i), shape, dt) for i in range(NL)], NL)
            bct = mkl("bct", [128, 4, 128], BF16)
            xtok = mkl("xtok", [128, 768], BF16)
            dtc = mkl("dtc", [128, 16], F32)
            szt = mkl("szt", [128, 512], BF16)
            ydn = mkl("ydn", [128, 512], F32)
            rings = [bct, xtok, dtc, szt, ydn]

            def setk(k_):
                for r_ in rings:
                    r_.cur = k_
            adt = mk("adt", [128, 8], F32)
            xdt = mk("xdt", [128, 512], BF16)
            rhsA = mk("rhsA", [128, 8, 128], F32)
            acs = mk("acs", [128, 16], F32)
            eA1 = rhsA
            Dm = mk("Dm", [128, 8, 128], F32)
            CBm = mk("CBm", [128, 2, 128], BF16)
            decs = mk("decs", [128, 8], F32)
            Lt = Dm
            Mt = mk("Mt", [128, 8, 128], BF16)
            Cs = mk("Cs", [128, 8, 128], BF16)
            wv = mk("wv", [128, 8], F32)
            lndt = mk("lndt", [128, 8], F32)
            xdtw = mk("xdtw", [128, 512], BF16)
            Ssb = mk("Ssb", [128, 512], F32)
            E = sb(es, "E", [128, 512], F32)
            Ebf = sb(es, "Ebf", [128, 512], BF16)
            tmpE = sb(es, "tmpE", [128, 512], F32)
            t1 = sb(es, "t1C", [128, 512], F32)
            t2 = sb(es, "t2C", [128, 512], F32)
            yn = sb(es, "ynC", [128, 512], BF16)
            junk = sb(es, "junkC", [128, 512], BF16)
            ssq = sb(es, "ssqC", [128, 2], F32)
            mxs = [sb(es, "mxs%d" % i, [128, 4, 512], BF16) for i in range(2)]
            A1 = [ps(es, "A1_%d" % i, [128, 2, 512], F32) for i in range(2)]
            cbs_ = ps(es, "cbsm", [128, 512], F32)
            cbp = [TB(cbs_.t[:, 0:256])] * 2
            smp = [TB(cbs_.t[:, 256:272]), TB(cbs_.t[:, 272:288])]
            for v_ in (cbp[0], smp[0], smp[1]):
                v_.b = cbs_.b
            yps = ps(es, "yps", [128, 512], F32)
            stp = ps(es, "stpC", [128, 512], F32)
            tpo = ps(es, "tpo", [128, 512], BF16)
            bcv = sd["BCT"].t.rearrange("(t p) s -> p t s", p=128)
            mixv = sd["MIXT"].t[512:1024, :].rearrange("(t p) s -> p t s", p=128)

            def h3(ap):
                return ap.rearrange("p (h d) -> p h d", h=8)

            import os
            _cstage = int(os.environ.get("CSTAGE", "0"))

            def loads(k0):
                ks = list(range(k0, min(k0 + G, nchk)))
                for k_ in ks:
                    setk(k_)
                    c = chunks[k_]
                    sl = k_ % NS
                    own = c < nch_own
                    r0 = c * 128
                    p.dma("sp", xtok[sl][:], sd["XTOK"][r0:r0 + 128, :], [sd["XTOK"].b], [xtok[sl].b])
                    p.dma("sp", dtc[sl][:], sd["DT"][r0:r0 + 128, :], [sd["DT"].b], [dtc[sl].b])
                    if own:
                        p.dma("sp", bct[sl][:], bcv[:, :, r0:r0 + 128], [sd["BCT"].b], [bct[sl].b])
                        if dr == 0:
                            p.dma("sp", szt[sl][:], sd["SZ"][r0:r0 + 128, :], [sd["SZ"].b], [szt[sl].b])
                            p.dma("sp", ydn[sl][:], sd["YDN"][r0:r0 + 128, :], [sd["YDN"].b], [ydn[sl].b])

            def front(k0):
                ks = list(range(k0, min(k0 + G, nchk)))
                for k_ in ks:
                    setk(k_)
                    sl = k_ % NS
                    dcol = dtc[sl][:, dr * 8:dr * 8 + 8]
                    p.tt("dve", adt[sl][:], dcol, aneg[:, dr * 8:dr * 8 + 8], ALU.mult, [dtc[sl].b, aneg.b], [adt[sl].b])
                    p.tt("dve", rhsA[sl][:], tri.unsqueeze(1).broadcast_to([128, 8, 128]),
                         adt[sl][:].unsqueeze(2).broadcast_to([128, 8, 128]), ALU.mult, [cst.b, adt[sl].b], [rhsA[sl].b])
                    p.act(lndt[sl][:], dcol, AF.Ln, [dtc[sl].b], [lndt[sl].b])
                if _cstage and _cstage <= 2:
                    return
                for k_ in ks:
                    setk(k_)
                    sl = k_ % NS
                    a1 = A1[k_ % 2]
                    for a_ in range(2):
                        p.mm(a1[:, a_, :], ones, rhsA[sl][:, a_ * 4:(a_ + 1) * 4, :].rearrange("p h l -> p (h l)"),
                             True, True, [cst.b, rhsA[sl].b], [a1.b], inc=(a_ == 1))
                    sm_ = smp[k_ % 2]
                    p.mm(sm_[:, 0:8], tri, adt[sl][:], True, True, [cst.b, adt[sl].b], [sm_.b], inc=False)
                    p.mm(sm_[:, 8:16], ones, adt[sl][:], True, True, [cst.b, adt[sl].b], [sm_.b], inc=True)
                for k_ in ks:
                    setk(k_)
                    c = chunks[k_]
                    sl = k_ % NS
                    own = c < nch_own
                    a1 = A1[k_ % 2]
                    a1v = a1.t[:, :, :].rearrange("p a (h l) -> p (a h) l", l=128)
                    p.copy("dve", acs[sl][:], smp[k_ % 2][:, 0:16], [smp[k_ % 2].b], [acs[sl].b])
                    if own:
                        p.tt("dve", lndt[sl][:], acs[sl][:, 0:8], lndt[sl][:], ALU.subtract, [acs[sl].b, lndt[sl].b],
                             [lndt[sl].b])
                        p.tt("dve", Dm[sl][:], a1v, lndt[sl][:].unsqueeze(2).broadcast_to([128, 8, 128]),
                             ALU.subtract, [a1.b, lndt[sl].b], [Dm[sl].b])
                for k_ in ks:
                    setk(k_)
                    c = chunks[k_]
                    sl = k_ % NS
                    own = c < nch_own
                    a1 = A1[k_ % 2]
                    a1v = a1.t[:, :, :].rearrange("p a (h l) -> p (a h) l", l=128)
                    if own and k_ > 0:
                        p.act(Cs[sl][:], a1v, AF.Exp, [a1.b, Dm[sl].b], [Cs[sl].b])
                    p.act(decs[sl][:], acs[sl][:, 8:16], AF.Exp, [acs[sl].b], [decs[sl].b])
                if _cstage and _cstage <= 3:
                    return
                owns = [k_ for k_ in ks if chunks[k_] < nch_own]
                for k_ in owns:
                    setk(k_)
                    sl = k_ % NS
                    p.ts("dve", Dm[sl][:], Dm[sl][:], 20.0, None, ALU.min, None, [Dm[sl].b], [Dm[sl].b])
                    p.act(Mt[sl][:], Dm[sl][:], AF.Exp, [Dm[sl].b], [Mt[sl].b])
                for k_ in owns:
                    setk(k_)
                    sl = k_ % NS
                    b_ = bct[sl]
                    cb_ = cbp[k_ % 2]
                    for g in range(2):
                        p.mm(cb_[:, g * 128:(g + 1) * 128], b_[:, g, :], b_[:, 2 + g, :], True, True, [b_.b], [cb_.b],
                             inc=(g == 1))
                    p.tt("dve", CBm[sl][:], cb_[:, 0:256].rearrange("p (g l) -> p g l", g=2),
                         tri.unsqueeze(1).broadcast_to([128, 2, 128]), ALU.mult, [cb_.b, cst.b], [CBm[sl].b])
                for k_ in owns:
                    setk(k_)
                    sl = k_ % NS
                    for g in range(2):
                        p.tt("dve", Mt[sl][:, 4 * g:4 * g + 4, :], Mt[sl][:, 4 * g:4 * g + 4, :],
                             CBm[sl][:, g, :].unsqueeze(1).broadcast_to([128, 4, 128]), ALU.mult,
                             [Mt[sl].b, CBm[sl].b], [Mt[sl].b])
                for k_ in owns:
                    setk(k_)
                    sl = k_ % NS
                    b_ = bct[sl]
                    if k_ > 0:
                        for g in range(2):
                            p.tt("dve", Cs[sl][:, 4 * g:4 * g + 4, :], Cs[sl][:, 4 * g:4 * g + 4, :],
                                 b_[:, 2 + g, :].unsqueeze(1).broadcast_to([128, 4, 128]), ALU.mult,
                                 [Cs[sl].b, b_.b], [Cs[sl].b])
                if _cstage and _cstage <= 4:
                    return
                sts = [k_ for k_ in ks if k_ != nchk - 1]
                for k_ in sts:
                    setk(k_)
                    sl = k_ % NS
                    p.tt("dve", wv[sl][:], acs[sl][:, 8:16], acs[sl][:, 0:8], ALU.subtract, [acs[sl].b], [wv[sl].b])
                    p.act(wv[sl][:], wv[sl][:], AF.Exp, [wv[sl].b], [wv[sl].b])
                for k_ in sts:
                    setk(k_)
                    sl = k_ % NS
                    p.tt("dve", wv[sl][:], wv[sl][:], dtc[sl][:, dr * 8:dr * 8 + 8], ALU.mult, [wv[sl].b, dtc[sl].b],
                         [wv[sl].b])
                    p.tt("dve", h3(xdtw[sl][:]), h3(xtok[sl][:, 0:512]), wv[sl][:].unsqueeze(2).broadcast_to([128, 8, 64]),
                         ALU.mult, [xtok[sl].b, wv[sl].b], [xdtw[sl].b])
                    for g in range(2):
                        p.mm(stp[:, g * 256:(g + 1) * 256], xtok[sl][:, 512 + g * 128:512 + (g + 1) * 128],
                             xdtw[sl][:, g * 256:(g + 1) * 256], True, True, [xtok[sl].b, xdtw[sl].b], [stp.b],
                             inc=(g == 1))
                    p.copy("act", Ssb[sl][:], stp[:], [stp.b], [Ssb[sl].b])

            def back(k0):
                for k_ in range(k0, min(k0 + G, nchk)):
                    setk(k_)
                    c = chunks[k_]
                    sl = k_ % NS
                    own = c < nch_own
                    r0 = c * 128
                    hs_ = k_ > 0
                    if own:
                        for h in range(8):
                            hs = slice(h * 64, (h + 1) * 64)
                            p.mm(yps[:, hs], Mt[sl][:, h, :], xtok[sl][:, hs], True, not hs_, [Mt[sl].b, xtok[sl].b],
                                 [yps.b], inc=(h == 7 and not hs_))
                            if hs_:
                                p.mm(yps[:, hs], Cs[sl][:, h, :], Ebf[:, hs], False, True, [Cs[sl].b, Ebf.b], [yps.b],
                                     inc=(h == 7))
                    if k_ < nchk - 1:
                        if hs_:
                            p.tt("dve", h3(tmpE[:]), h3(E[:]),
                                 decs[sl][:].unsqueeze(2).broadcast_to([128, 8, 64]), ALU.mult,
                                 [E.b, decs[sl].b], [tmpE.b])
                            p.tt("dve", E[:], tmpE[:], Ssb[sl][:], ALU.add, [tmpE.b, Ssb[sl].b], [E.b])
                        else:
                            p.copy("pool", E[:], Ssb[sl][:], [Ssb[sl].b], [E.b])
                        p.copy("act", Ebf[:], E[:], [E.b], [Ebf.b])
                    if own:
                        if dr == 1:
                            p.copy("act", ydn[sl][:], yps[:], [yps.b], [ydn[sl].b])
                            p.dma("sp", sd["YDN"][r0:r0 + 128, :], ydn[sl][:], [ydn[sl].b], [sd["YDN"].b])
                        else:
                            p.tt("pool", t1[:], xtok[sl][:, 0:512], prow[:, ROW_DSKIP:ROW_DSKIP + 512], ALU.mult,
                                 [xtok[sl].b, prow.b], [t1.b])
                            p.tt("dve", t2[:], yps[:], ydn[sl][:], ALU.add, [yps.b, ydn[sl].b], [t2.b])
                            p.tt("pool", t1[:], t1[:], t2[:], ALU.add, [t1.b, t2.b], [t1.b])
                            p.tt("pool", t1[:], t1[:], szt[sl][:], ALU.mult, [t1.b, szt[sl].b], [t1.b])
                            p.act(junk[:], t1[:], AF.Square, [t1.b], [junk.b, ssq.b], accum=ssq[:, 0:1])
                            p.act(ssq[:, 1:2], ssq[:, 0:1], AF.Ln, [ssq.b, epsc.b], [ssq.b], scale=1.0 / 512,
                                  bias=epsc[:, 0:1])
                            p.act(ssq[:, 1:2], ssq[:, 1:2], AF.Exp, [ssq.b], [ssq.b], scale=-0.5)
                            p.stt(yn[:], t1[:], ssq[:, 1:2], prow[:, ROW_GSSD:ROW_GSSD + 512], ALU.mult, ALU.mult,
                                  [t1.b, ssq.b, prow.b], [yn.b])
                            for t in range(4):
                                p.tr(tpo[:, t * 128:(t + 1) * 128], yn[:, t * 128:(t + 1) * 128], ident,
                                     [yn.b, cstb.b], [tpo.b], inc=(t == 3))
                            mx = mxs[(c // 4) % 2]
                            p.copy("act", mx[:, :, (c % 4) * 128:(c % 4 + 1) * 128],
                                   tpo[:].rearrange("p (t l) -> p t l", t=4), [tpo.b], [mx.b])
                            if c % 4 == 3:
                                t0 = (c // 4) * 512
                                p.dma("sp", mixv[:, :, t0:t0 + 512], mx[:], [mx.b], [sd["MIXT"].b])

            starts = list(range(0, nchk, G))
            import os
            _lim = int(os.environ.get("CLIM", "0"))
            _nob = int(os.environ.get("CNOBACK", "0"))
            if _lim:
                starts = starts[:_lim] if dr == 1 else []
            for k0 in starts[:3]:
                loads(k0)
            for k0 in starts[:2]:
                front(k0)
            for gi, k0 in enumerate(starts):
                if gi + 3 < len(starts):
                    loads(starts[gi + 3])
                if gi + 2 < len(starts):
                    front(starts[gi + 2])
                if not _nob:
                    back(k0)
        p.barrier()


def phase_D(nc, p, sb, ps, si, so, off, x_own, y_out, sd, mod_d, mc, w_out_b, w_gu_b, w_dn_b, cstb, epsc):
    with contextlib.ExitStack() as es:
        wout = sb(es, "wout", [128, 8, D], BF16)
        wdn = sb(es, "wdn", [128, 22, D], BF16)
        NWG = 4
        wgu = [sb(es, "wgu%d" % i, [128, 8, 512], BF16) for i in range(NWG)]
        g1b = sb(es, "g1b", [128, D], F32)
        g2b = sb(es, "g2b", [128, D], F32)
        mixT = sb(es, "mixT", [128, 8, 512], BF16)
        xt = [sb(es, "xD%d" % i, [128, D], F32) for i in range(4)]
        xn2 = sb(es, "xn2", [128, 4, D], BF16)
        h2T = sb(es, "h2T", [128, 8, 512], BF16)
        aT = sb(es, "aT", [128, 22, 512], BF16)
        sg = [sb(es, "sg%d" % i, [128, 512], F32) for i in range(2)]
        tmp = [sb(es, "tmpD%d" % i, [128, 512], F32) for i in range(2)]
        yo = [sb(es, "yo%d" % i, [128, D], F32) for i in range(2)]
        junk = sb(es, "junkD", [128, D], BF16)
        ssq = sb(es, "ssqD", [128, 8], F32)
        od = [ps(es, "odD%d" % i, [128, 512], F32) for i in range(2)]
        tp = [ps(es, "tpD%d" % i, [128, 512], BF16) for i in range(2)]
        gu = [ps(es, "guD%d" % i, [128, 512], F32) for i in range(4)]
        ident = cstb[:, C_ID * 128:(C_ID + 1) * 128]

        wv = w_out_b.t.rearrange("(kc p) n -> p kc n", p=128)
        for kc in range(0, 8, 4):
            p.dma("sp", wout[:, kc:kc + 4, :], wv[:, kc:kc + 4, :], [w_out_b.b], [wout.b])
        wv = w_dn_b.t.rearrange("(kc p) n -> p kc n", p=128)
        for kc in range(0, 22, 4):
            k1 = min(22, kc + 4)
            p.dma("sp", wdn[:, kc:k1, :], wv[:, kc:k1, :], [w_dn_b.b], [wdn.b])
        p.dma("sp", g1b[:], mod_d[si:si + 1, 2 * D:3 * D].broadcast_to([128, D]), [mod_d.b], [g1b.b])
        p.dma("sp", g2b[:], mod_d[si:si + 1, 5 * D:6 * D].broadcast_to([128, D]), [mod_d.b], [g2b.b])
        wguv = w_gu_b.t.rearrange("(kc p) n -> p kc n", p=128)
        mixv = sd["MIXT"].t.rearrange("(kc p) s -> p kc s", p=128)
        wi = 0
        oi = 0
        yi = 0
        for mt in range(so // 512):
            t0 = mt * 512
            p.dma("sp", mixT[:], mixv[:, :, t0:t0 + 512], [sd["MIXT"].b], [mixT.b])
            for sub in range(4):
                r0 = off + t0 + sub * 128
                p.dma("sp", xt[sub][:], x_own[r0:r0 + 128, :], (), [xt[sub].b])
            for sub in range(4):
                x_ = xt[sub]
                for half in range(2):
                    acc = od[oi % 2]
                    tm = tmp[oi % 2]
                    oi += 1
                    for kc in range(8):
                        p.mm(acc[:], mixT[:, kc, sub * 128:(sub + 1) * 128], wout[:, kc, half * 512:(half + 1) * 512],
                             kc == 0, kc == 7, [mixT.b, wout.b], [acc.b], inc=(kc == 7))
                    hs = slice(half * 512, (half + 1) * 512)
                    p.tt("dve", tm[:], acc[:], g1b[:, hs], ALU.mult, [acc.b, g1b.b], [tm.b])
                    p.tt("pool", x_[:, hs], x_[:, hs], tm[:], ALU.add, [x_.b, tm.b], [x_.b])
                p.act(junk[:], x_[:], AF.Square, [x_.b], [junk.b, ssq.b], accum=ssq[:, sub:sub + 1])
                p.act(ssq[:, 4 + sub:5 + sub], ssq[:, sub:sub + 1], AF.Sqrt, [ssq.b, epsc.b], [ssq.b],
                      scale=1.0 / D, bias=epsc[:, 0:1])
                p.op("dve", lambda e, s_=sub: e.reciprocal(out=ssq[:, 4 + s_:5 + s_], in_=ssq[:, 4 + s_:5 + s_]),
                     [ssq.b], [ssq.b])
                p.act(xn2[:, sub, :], x_[:], AF.Copy, [x_.b, ssq.b], [xn2.b], scale=ssq[:, 4 + sub:5 + sub])
            for kc in range(8):
                tpp = tp[kc % 2]
                for sub in range(4):
                    p.tr(tpp[:, sub * 128:(sub + 1) * 128], xn2[:, sub, kc * 128:(kc + 1) * 128], ident,
                         [xn2.b, cstb.b], [tpp.b], inc=(sub == 3))
                p.ts("dve", h2T[:, kc, :], tpp[:], mc[:, 2, kc:kc + 1], mc[:, 3, kc:kc + 1], ALU.mult, ALU.add,
                     [tpp.b, mc.b], [h2T.b])
            gi = 0
            for b in range(11):
                w = wgu[wi % NWG]
                wi += 1
                p.dma("sp", w[:], wguv[:, :, b * 512:(b + 1) * 512], [w_gu_b.b], [w.b])
                for t in range(2):
                    ag = gu[gi % 4]
                    au = gu[(gi + 1) % 4]
                    s_ = sg[(gi // 2) % 2]
                    gi += 2
                    for kc in range(8):
                        p.mm(ag[:], w[:, kc, t * 128:(t + 1) * 128], h2T[:, kc, :], kc == 0, kc == 7,
                             [w.b, h2T.b], [ag.b], inc=(kc == 7))
                    for kc in range(8):
                        p.mm(au[:], w[:, kc, 256 + t * 128:256 + (t + 1) * 128], h2T[:, kc, :], kc == 0, kc == 7,
                             [w.b, h2T.b], [au.b], inc=(kc == 7))
                    p.act(s_[:], ag[:], AF.Silu, [ag.b], [s_.b])
                    p.tt("dve", aT[:, 2 * b + t, :], s_[:], au[:], ALU.mult, [s_.b, au.b], [aT.b])
            for sub in range(4):
                x_ = xt[sub]
                y_ = yo[yi % 2]
                yi += 1
                for half in range(2):
                    acc = od[oi % 2]
                    tm = tmp[oi % 2]
                    oi += 1
                    for kc in range(22):
                        p.mm(acc[:], aT[:, kc, sub * 128:(sub + 1) * 128], wdn[:, kc, half * 512:(half + 1) * 512],
                             kc == 0, kc == 21, [aT.b, wdn.b], [acc.b], inc=(kc == 21))
                    hs = slice(half * 512, (half + 1) * 512)
                    p.tt("dve", tm[:], acc[:], g2b[:, hs], ALU.mult, [acc.b, g2b.b], [tm.b])
                    p.tt("pool", y_[:, hs], x_[:, hs], tm[:], ALU.add, [x_.b, tm.b], [y_.b])
                r0 = off + t0 + sub * 128
                p.dma("sp", y_out[r0:r0 + 128, :], y_[:], [y_.b], [y_out.b])


def _consts():
    c = np.zeros((128, NCONST, 128), np.float32)
    k = np.arange(128)[:, None]
    m = np.arange(128)[None, :]
    c[:, C_ID] = (k == m)
    c[:, C_BONES] = ((k // 64) == (m // 64)) / 64.0
    partner = np.where((np.arange(128) % 32) < 16, np.arange(128) + 16, np.arange(128) - 16)
    c[:, C_RPERM] = (k == partner[None, :])
    c[:, C_TRIUP] = (k <= m)
    c[:, C_TRIDN] = (k >= m)
    c[:, C_ONES] = 1.0
    c[64, C_SEL, 0:64] = 1.0
    c[0, C_SELB, 64:128] = 1.0
    return c.reshape(128, NCONST * 128)


def _rope_tab(S, rev):
    t = np.arange(S)
    if rev:
        t = t[::-1]
    rows = (t // 64).astype(np.float32)
    cols = (t % 64).astype(np.float32)
    freqs = (np.float32(10000.0) ** (-np.arange(16, dtype=np.float32) / np.float32(16))).astype(np.float32)
    tab = np.zeros((2, 64, S), np.float32)
    for d in range(64):
        pos = rows if d < 32 else cols
        ang = (pos * freqs[d % 16]).astype(np.float32)
        sign = -1.0 if (d % 32) < 16 else 1.0
        tab[0, d] = np.cos(ang)
        tab[1, d] = sign * np.sin(ang)
    return tab


def prep_core(c, I, SP, SO, NP=2):
    rev = (c % 2) == 1
    f32 = np.float32
    xs = [np.asarray(I["x_prompt"][NP * c + i]) for i in range(NP)]
    smp = np.asarray(I["x_sample"][c // 2])
    if rev:
        xs = [x[::-1] for x in xs]
        smp = smp[::-1]
    x_own = np.ascontiguousarray(np.concatenate(xs + [smp[:SO]], axis=0), dtype=f32)
    x_ctx = np.ascontiguousarray(smp[SO:], dtype=f32)
    cv = np.stack([I["c_prompt"][NP * c + i] for i in range(NP)] + [I["c_sample"][c // 2]], axis=0)
    cT = np.ascontiguousarray(cv.T.reshape(8, 128, NP + 1).transpose(1, 0, 2), dtype=f32)
    wi = np.asarray(I["w_in"][0])
    q, k, v, z, xbc = wi[:, 0:512], wi[:, 512:640], wi[:, 640:768], wi[:, 768:1280], wi[:, 1280:2304]
    dtf, dtb = wi[:, 2304:2312], wi[:, 2312:2320]
    up, dn = (dtb, dtf) if rev else (dtf, dtb)
    w_in = np.ascontiguousarray(np.concatenate([k, q, xbc, z, v, up, dn], axis=1), dtype=f32)
    wg = np.asarray(I["w_gate_up"][0])
    blocks = []
    for b in range(11):
        blocks += [wg[:, b * 256:(b + 1) * 256], wg[:, DFF + b * 256:DFF + (b + 1) * 256]]
    w_gu = np.ascontiguousarray(np.concatenate(blocks, axis=1), dtype=f32)
    pcol = np.zeros((128, NCOLP), f32)
    pcol[:, COL_QG] = np.tile(I["q_norm_g"][0], 2)
    pcol[:, COL_KG] = np.tile(I["k_norm_g"][0], 2)
    pcol[:, COL_GMIX:COL_GMIX + 8] = I["norm_mix_g"][0].reshape(8, 128).T
    pcol[:, COL_GFFN:COL_GFFN + 8] = I["norm_ffn_g"][0].reshape(8, 128).T
    pcol[:, COL_CB:COL_CB + 8] = I["conv_b"][0].reshape(8, 128).T
    cw = np.asarray(I["conv_w"][0])
    if rev:
        cw = cw[::-1]
    pcol[:, COL_CW:COL_CW + 40] = cw.reshape(5, 8, 128).transpose(2, 1, 0).reshape(128, 40)
    prow1 = np.zeros((NROWP,), f32)
    bu, bd = (I["dt_bias_bwd"][0], I["dt_bias_fwd"][0]) if rev else (I["dt_bias_fwd"][0], I["dt_bias_bwd"][0])
    au, ad = (I["a_log_bwd"][0], I["a_log_fwd"][0]) if rev else (I["a_log_fwd"][0], I["a_log_bwd"][0])
    prow1[ROW_DTB:ROW_DTB + 16] = np.concatenate([bu, bd])
    prow1[ROW_ALOG:ROW_ALOG + 16] = np.concatenate([au, ad])
    prow1[ROW_DSKIP:ROW_DSKIP + 512] = np.repeat(I["d_skip"][0], 64)
    prow1[ROW_GSSD:ROW_GSSD + 512] = I["ssd_norm_g"][0]
    prow = np.ascontiguousarray(np.broadcast_to(prow1, (128, NROWP)), dtype=f32)
    sel3 = np.zeros((NP + 1, (NP + 1) * 128), f32)
    for s in range(NP + 1):
        sel3[s, s * 128:(s + 1) * 128] = 1.0
    return {
        "x_own": x_own, "x_ctx": x_ctx, "cT": cT,
        "w_mod": np.ascontiguousarray(I["w_mod"][0], dtype=f32),
        "b_mod3": np.ascontiguousarray(np.broadcast_to(I["b_mod"][0], (NP + 1, 6 * D)), dtype=f32),
        "w_in": w_in, "w_out": np.ascontiguousarray(I["w_out"][0], dtype=f32), "w_gu": w_gu,
        "w_dn": np.ascontiguousarray(I["w_down"][0], dtype=f32),
        "pcol": pcol, "prow": prow, "consts": _consts(), "sel3": sel3,
        "rope_p": _rope_tab(SP, rev), "rope_s": _rope_tab(2 * SO, rev),
    }


def run(I, SP, SO, cores=range(8), NP=2, debug=False, phases="ABCD"):
    nc = build(SP, SO, SO, NP=NP, debug=debug, phases=phases)
    cores = list(cores)
    in_maps = [prep_core(c, I, SP, SO, NP) for c in cores]
    res = run_bass_kernel_spmd(nc, in_maps, core_ids=list(range(len(cores))))
    return res.results


def kernel(**I):
    I = {k: np.asarray(v) for k, v in I.items()}
    B, SP, _ = I["x_prompt"].shape
    DB, DS, _ = I["x_sample"].shape
    SO = DS // 2
    NP = B // 8
    results = run(I, SP, SO, range(8), NP)
    yp = np.zeros((B, SP, D), np.float32)
    ys = np.zeros((DB, DS, D), np.float32)
    for c in range(8):
        y = results[c]["y_own"]
        rev = (c % 2) == 1
        for i in range(NP):
            seg = y[i * SP:(i + 1) * SP]
            yp[NP * c + i] = seg[::-1] if rev else seg
        seg = y[NP * SP:NP * SP + SO]
        if rev:
            ys[c // 2, SO:] = seg[::-1]
        else:
            ys[c // 2, :SO] = seg
    return (yp, ys)
```
